# Optimizing a Trainium2 kernel written in Bass

```python
import jax, jax.numpy as jnp
from jax import lax
import numpy as np

D_MODEL = 2048
BATCH = 4
SEQ = 2048
DEPTH = 2

CHUNK = 64
Q_BLOCK = 128
EPS = 1e-6
MIX_WIDTH = D_MODEL
N_GROUPS = 4
GROUP_WIDTH = MIX_WIDTH // N_GROUPS
CONV_WIDTH = 3
POOL_WINDOWS = (2, 4, 8, 16)
POOL_GROUP_DIM = GROUP_WIDTH // len(POOL_WINDOWS)
GLA_HEADS = 4
GLA_VALUE_DIM = GROUP_WIDTH // GLA_HEADS
GLA_KEY_DIM = GLA_VALUE_DIM // 2
GLA_RANK = 16
GLA_TAU = 16.0
FOX_HEADS = 4
FOX_HEAD_DIM = GROUP_WIDTH // FOX_HEADS
FOX_GATE_BIAS = 3.0
D_FF = -(-8 * D_MODEL // (3 * 256)) * 256

IN_SPLITS = (
    GROUP_WIDTH, GROUP_WIDTH, GROUP_WIDTH,
    GROUP_WIDTH,
    GLA_HEADS * GLA_KEY_DIM, GLA_HEADS * GLA_KEY_DIM, GROUP_WIDTH, GROUP_WIDTH, GLA_RANK,
    GROUP_WIDTH, GROUP_WIDTH, GROUP_WIDTH, FOX_HEADS,
)
IN_COLS = sum(IN_SPLITS)

kernel_name = 'hybrid_conv_pool_gla_fox_trunk'


def _split_points():
    pts, acc = [], 0
    for s in IN_SPLITS[:-1]:
        acc += s
        pts.append(acc)
    return pts


def rmsnorm(x, g):
    xf = x.astype(jnp.float32)
    y = xf * lax.rsqrt(jnp.mean(xf * xf, axis=-1, keepdims=True) + EPS)
    return (y * g.astype(jnp.float32)).astype(x.dtype)


def short_gated_conv(b_gate, c_gate, h, conv_w):
    u = c_gate * h
    y = lax.conv_general_dilated(u, conv_w.astype(u.dtype), window_strides=(1,),
                                 padding=[(CONV_WIDTH - 1, 0)],
                                 dimension_numbers=('NWC', 'WIO', 'NWC'),
                                 feature_group_count=u.shape[-1])
    return b_gate * y


def multiscale_pool(p, pool_w, pool_scale):
    Bn, S, G = p.shape
    P = POOL_WINDOWS[-1]
    pg = p.astype(jnp.float32).reshape(Bn, S, len(POOL_WINDOWS), POOL_GROUP_DIM)
    cs = jnp.cumsum(jnp.pad(pg, ((0, 0), (P + 1, 0), (0, 0), (0, 0))), axis=1)
    t = jnp.arange(S, dtype=jnp.float32)
    diffs = []
    for gi, w in enumerate(POOL_WINDOWS):
        wsum = cs[:, P + 1:P + 1 + S, gi] - cs[:, P + 1 - w:P + 1 - w + S, gi]
        count = jnp.minimum(t + 1.0, float(w))[:, None]
        diffs.append(wsum / count - pg[:, :, gi])
    d = jnp.stack(diffs, axis=2)
    y = jnp.einsum('bsgc,gcd->bsgd', d, pool_w.astype(jnp.float32))
    return (y.reshape(Bn, S, G) * pool_scale.astype(jnp.float32)).astype(p.dtype)


def gated_linear_attention(q, k, v, g, a_low, w_decay, b_decay, out_g):
    Bn, S, _ = q.shape
    H, K, V = GLA_HEADS, GLA_KEY_DIM, GLA_VALUE_DIM
    N = S // CHUNK
    f32 = jnp.float32

    def chunks(t, d):
        return t.astype(f32).reshape(Bn, N, CHUNK, H, d).transpose(1, 0, 2, 3, 4)

    log_a = jax.nn.log_sigmoid((a_low @ w_decay + b_decay).astype(f32)) / GLA_TAU
    cum = jnp.cumsum(chunks(log_a, K), axis=2)
    qs = chunks(q, K) * (K ** -0.5)
    ks = chunks(k, K)
    vs = chunks(v, V)

    def step(state, inp):
        qc, kc, vc, cc = inp
        decay = jnp.exp(-jnp.abs(cc[:, :, None] - cc[:, None, :]))
        scores = jnp.einsum('bthk,btshk,bshk->bhts', qc, decay, kc)
        o_intra = jnp.einsum('bhts,bshv->bthv', scores, vc)
        o_inter = jnp.einsum('bthk,bhkv->bthv', qc * jnp.exp(cc), state)
        last = cc[:, -1]
        k_dec = kc * jnp.exp(last[:, None] - cc)
        state = state * jnp.exp(last)[..., None] + jnp.einsum('bshk,bshv->bhkv', k_dec, vc)
        return state, o_intra + o_inter

    state0 = jnp.zeros((Bn, H, K, V), f32)
    _, o = lax.scan(step, state0, (qs, ks, vs, cum))
    o = o.transpose(1, 0, 2, 3, 4).reshape(Bn, S, H, V)
    o = rmsnorm(o, out_g) * jax.nn.silu(g.astype(f32)).reshape(Bn, S, H, V)
    return o.reshape(Bn, S, H * V).astype(q.dtype)


def forgetting_attention(q, k, v, f_logit, q_g, k_g, b_f):
    Bn, S, _ = q.shape
    H, Dh = FOX_HEADS, FOX_HEAD_DIM
    q = rmsnorm(q.reshape(Bn, S, H, Dh), q_g)
    k = rmsnorm(k.reshape(Bn, S, H, Dh), k_g)
    v = v.reshape(Bn, S, H, Dh)
    log_f = jax.nn.log_sigmoid((f_logit + b_f).astype(jnp.float32))
    F = jnp.cumsum(log_f, axis=1).transpose(0, 2, 1)
    n_blk = S // Q_BLOCK
    q_blocks = q.reshape(Bn, n_blk, Q_BLOCK, H, Dh).transpose(1, 0, 2, 3, 4)
    F_blocks = F.reshape(Bn, H, n_blk, Q_BLOCK).transpose(2, 0, 1, 3)
    key_pos = jnp.arange(S)
    scale = Dh ** -0.5

    def attend_block(args):
        q_blk, F_blk, blk = args
        logits = jnp.einsum('bqhd,bshd->bhqs', q_blk, k).astype(jnp.float32) * scale
        logits = logits + F_blk[..., :, None] - F[..., None, :]
        q_pos = blk * Q_BLOCK + jnp.arange(Q_BLOCK)
        logits = jnp.where(key_pos[None, :] <= q_pos[:, None], logits, -jnp.inf)
        p = jax.nn.softmax(logits, axis=-1).astype(v.dtype)
        return jnp.einsum('bhqs,bshd->bqhd', p, v)

    out = lax.map(attend_block, (q_blocks, F_blocks, jnp.arange(n_blk)))
    return out.transpose(1, 0, 2, 3, 4).reshape(Bn, S, H * Dh)


def hybrid_mixer(u, w_in, conv_w, pool_w, pool_scale, gla_w_decay, gla_b_decay, gla_out_g,
                 fox_q_g, fox_k_g, fox_b_f, w_out):
    z = u @ w_in
    (cb, cc, ch, pu, gq, gk, gv, gg, ga, fq, fk, fv, ff) = jnp.split(z, _split_points(), axis=-1)
    y_a = short_gated_conv(cb, cc, ch, conv_w)
    y_b = multiscale_pool(pu, pool_w, pool_scale)
    y_c = gated_linear_attention(gq, gk, gv, gg, ga, gla_w_decay, gla_b_decay, gla_out_g)
    y_d = forgetting_attention(fq, fk, fv, ff, fox_q_g, fox_k_g, fox_b_f)
    y = jnp.concatenate([y_a.astype(u.dtype), y_b.astype(u.dtype),
                         y_c.astype(u.dtype), y_d.astype(u.dtype)], axis=-1)
    return y @ w_out


def swiglu_ffn(u, w_gate, w_up, w_down):
    return (jax.nn.silu(u @ w_gate) * (u @ w_up)) @ w_down


def setup_inputs(seed: int = 0) -> dict:
    key = jax.random.key(seed)
    ks = jax.random.split(key, 17)
    f32 = jnp.float32

    def nrm(k, shape, scale):
        return jax.random.normal(k, shape, f32) * scale

    G = GROUP_WIDTH
    return {
        'x': nrm(ks[0], (BATCH, SEQ, D_MODEL), 1.0),
        'norm_mix_g': 1.0 + nrm(ks[1], (DEPTH, D_MODEL), 0.02),
        'w_in': nrm(ks[2], (DEPTH, D_MODEL, IN_COLS), D_MODEL ** -0.5),
        'conv_w': nrm(ks[3], (DEPTH, CONV_WIDTH, 1, G), CONV_WIDTH ** -0.5),
        'pool_w': nrm(ks[4], (DEPTH, len(POOL_WINDOWS), POOL_GROUP_DIM, POOL_GROUP_DIM), POOL_GROUP_DIM ** -0.5),
        'pool_scale': 1.0 + nrm(ks[5], (DEPTH, G), 0.02),
        'gla_w_decay': nrm(ks[6], (DEPTH, GLA_RANK, GLA_HEADS * GLA_KEY_DIM), GLA_RANK ** -0.5),
        'gla_b_decay': nrm(ks[7], (DEPTH, GLA_HEADS * GLA_KEY_DIM), 0.1),
        'gla_out_g': 1.0 + nrm(ks[8], (DEPTH, GLA_VALUE_DIM), 0.02),
        'fox_q_g': 1.0 + nrm(ks[9], (DEPTH, FOX_HEAD_DIM), 0.02),
        'fox_k_g': 1.0 + nrm(ks[10], (DEPTH, FOX_HEAD_DIM), 0.02),
        'fox_b_f': FOX_GATE_BIAS + nrm(ks[11], (DEPTH, FOX_HEADS), 0.1),
        'w_out': nrm(ks[12], (DEPTH, MIX_WIDTH, D_MODEL), MIX_WIDTH ** -0.5),
        'norm_ffn_g': 1.0 + nrm(ks[13], (DEPTH, D_MODEL), 0.02),
        'w_gate': nrm(ks[14], (DEPTH, D_MODEL, D_FF), D_MODEL ** -0.5),
        'w_up': nrm(ks[15], (DEPTH, D_MODEL, D_FF), D_MODEL ** -0.5),
        'w_down': nrm(ks[16], (DEPTH, D_FF, D_MODEL), D_FF ** -0.5),
    }


def reference(x, norm_mix_g, w_in, conv_w, pool_w, pool_scale, gla_w_decay, gla_b_decay, gla_out_g,
              fox_q_g, fox_k_g, fox_b_f, w_out, norm_ffn_g, w_gate, w_up, w_down):
    h = x
    for l in range(DEPTH):
        u = rmsnorm(h, norm_mix_g[l])
        h = h + hybrid_mixer(u, w_in[l], conv_w[l], pool_w[l], pool_scale[l], gla_w_decay[l],
                             gla_b_decay[l], gla_out_g[l], fox_q_g[l], fox_k_g[l], fox_b_f[l], w_out[l])
        u = rmsnorm(h, norm_ffn_g[l])
        h = h + swiglu_ffn(u, w_gate[l], w_up[l], w_down[l])
    return h
```

```python
import contextlib
import numpy as np
import concourse.bass as bass
import concourse.mybir as mybir
from concourse.bass_utils import run_bass_kernel_spmd

F32 = mybir.dt.float32
BF16 = mybir.dt.bfloat16
AF = mybir.ActivationFunctionType
ALU = mybir.AluOpType

import os, time as _time
NCORES = 8
NRUN = int(os.environ.get('KCORES', '8'))
DRUN = int(os.environ.get('KDEPTH', '2'))
KDBG = bool(os.environ.get('KDBG'))
KSTOP = os.environ.get('KSTOP', '')


class StopEmit(Exception):
    pass
D = 2048
T = 1024
NT = 8
DEPTH = 2
INC = 5140
DFF = 5632
EPS = 1e-6
NEG = -30000.0
GLA_SCHED = [[], [('S', 0), ('A', 0)], [('B', 0), ('A', 1)], [('B', 1), ('A', 2)], [('B', 2), ('A', 3)], [('B', 3), ('A', 4)],
             [('B', 4), ('A', 5)], [('B', 5), ('A', 6), ('B', 6), ('A', 7), ('B', 7)]]

PK_CW = 0
PK_PSC = 12
PK_BD = 16
PK_QG = 18
PK_KG = 19
PK_BF = 20
PK_GOG = 52
PK_WD = 180
PK_GM = 436
PK_GF = 452
NPK = 468

CB_ID = 0
CB_ONE = 128
CB_ML = 256
CB_MU = 384
CB_MN = 512
CB_CM = 640
CB_EV = 1664
NCB = 2688
CF_TRI = 0
CF_ONE = 128
NCF = 256
PC_ROLE = 0
PC_NEGR = 1
PC_INVC = 2
NPC = 66

XF_NLF = 0
XF_A = 32
XF_B = 40
XF_S = 104
NXF = 360


DBG_OUTS = []


class Prog:
    ENGS = ("pe", "act", "dve", "pool", "sp")

    def __init__(self, nc, stack):
        self.nc = nc
        self.stack = stack
        self.h = {"pe": nc.tensor, "act": nc.scalar, "dve": nc.vector, "pool": nc.gpsimd, "sp": nc.sync}
        self.q = {e: [] for e in self.ENGS}
        self.semobj = {e: stack.enter_context(nc.semaphore("s_" + e)) for e in self.ENGS}
        self.cnt = {e: 0 for e in self.ENGS}
        self.known = {e: {} for e in self.ENGS}
        self.lastw = {}
        self.readers = {}
        self.nops = 0

    def dsem(self, name):
        if name not in self.semobj:
            self.semobj[name] = self.stack.enter_context(self.nc.semaphore("d_" + name))
            self.cnt[name] = 0
        return name

    def _deps(self, eng, reads, writes, sync_self):
        deps = {}

        def add(tok):
            if tok is None:
                return
            k, v = tok
            if deps.get(k, 0) < v:
                deps[k] = v

        for r in reads:
            add(self.lastw.get(r))
        for w in writes:
            add(self.lastw.get(w))
            for tok in self.readers.get(w, {}).items():
                add(tok)
        waits = []
        for k, v in deps.items():
            if k == eng and not sync_self:
                continue
            if self.known[eng].get(k, 0) >= v:
                continue
            self.known[eng][k] = v
            waits.append((k, v))
        return waits

    def _record(self, tok, reads, writes):
        k, v = tok
        for r in reads:
            d = self.readers.setdefault(r, {})
            if d.get(k, 0) < v:
                d[k] = v
        for w in writes:
            self.lastw[w] = tok
            self.readers[w] = {}

    def op(self, eng, fn, reads=(), writes=(), sync_self=None):
        if sync_self is None:
            sync_self = eng != "pe"
        writes = list(writes) + [r for r in reads if r.startswith("ps")]
        reads = [r for r in reads if not r.startswith("ps")]
        waits = self._deps(eng, reads, writes, sync_self)
        self.cnt[eng] += 1
        tok = (eng, self.cnt[eng])
        self.q[eng].append((waits, fn, eng, 1))
        self._record(tok, reads, writes)
        self.nops += 1
        return tok

    def dma(self, queue, fn, dsem, reads=(), writes=(), inc=16):
        self.dsem(dsem)
        waits = self._deps(queue, reads, writes, True)
        self.cnt[dsem] += inc
        tok = (dsem, self.cnt[dsem])
        self.q[queue].append((waits, fn, dsem, inc))
        self._record(tok, reads, writes)
        return tok

    def fence(self):
        snap = {k: v for k, v in self.cnt.items() if v > 0 and not k.startswith("cc_")}
        for e in self.ENGS:
            if e == "pool":
                continue
            waits = []
            for k, v in snap.items():
                if k == e:
                    continue
                if self.known[e].get(k, 0) >= v:
                    continue
                self.known[e][k] = v
                waits.append((k, v))
            if waits:
                self.q[e].append((waits, None, None, 0))

    def final_wait(self, eng, toks):
        waits = [t for t in toks if t is not None]
        self.q[eng].append((waits, None, None, 0))

    def build(self):
        with self.nc.Block() as block:
            secs = {"pe": block.tensor, "act": block.scalar, "dve": block.vector, "pool": block.gpsimd, "sp": block.sync}
            for e in self.ENGS:
                items = self.q[e]
                semobj = self.semobj

                def body(engh, items=items):
                    for waits, fn, semk, inc in items:
                        for k, v in waits:
                            engh.wait_ge(semobj[k], v)
                        if fn is not None:
                            ins = fn(engh)
                            ins.then_inc(semobj[semk], inc)

                secs[e](body)


def build_program():
    nc = bass.Bass("TRN2", target_bir_lowering=False)
    dt_in = lambda name, shape: nc.dram_tensor(name, shape, F32, kind="ExternalInput").ap()
    x_d = dt_in("x", [T, D])
    w_in_d = dt_in("w_in", [DEPTH, D, INC])
    if os.environ.get('KSMALLW'):
        w_out_d = w_gate_d = w_up_d = w_down_d = None
    else:
        w_out_d = dt_in("w_out", [DEPTH, D, D])
        w_gate_d = dt_in("w_gate", [DEPTH, D, DFF])
        w_up_d = dt_in("w_up", [DEPTH, D, DFF])
        w_down_d = dt_in("w_down", [DEPTH, DFF, D])
    pk_d = dt_in("pkp", [DEPTH, 128, NPK])
    poolw_d = dt_in("poolwp", [DEPTH, 128, 512])
    cb_d = dt_in("cbf", [128, NCB])
    cf_d = dt_in("cf32", [128, NCF])
    pc_d = dt_in("pcc", [128, NPC])
    out_d = nc.dram_tensor("out", [T, D], F32, kind="ExternalOutput").ap()
    hs_d = nc.dram_tensor("hs", [T, D], F32).ap()
    if KDBG:
        dbg_uT = nc.dram_tensor("dbg_uT", [128, 16 * T], BF16, kind="ExternalOutput").ap()
        dbg_yT = nc.dram_tensor("dbg_yT", [128, 16 * T], BF16, kind="ExternalOutput").ap()
        dbg_h1 = nc.dram_tensor("dbg_h1", [128, NT * D], F32, kind="ExternalOutput").ap()
    xs_kv = nc.dram_tensor("xs_kv", [128, 8192], BF16).ap()
    xr_kv = nc.dram_tensor("xr_kv", [256, 8192], BF16).ap()
    xs_f1 = nc.dram_tensor("xs_f1", [128, XF_S], F32).ap()
    xr_f1 = nc.dram_tensor("xr_f1", [256, XF_S], F32).ap()
    xs_f2 = nc.dram_tensor("xs_f2", [128, 256], F32).ap()
    xr_f2 = nc.dram_tensor("xr_f2", [256, 256], F32).ap()

    with contextlib.ExitStack() as stack:
        P = Prog(nc, stack)
        dbg_n = [0]

        def dbg(name, ap, shape, dt, reads):
            if not KDBG:
                return
            d = nc.dram_tensor("dbg_" + name, list(shape), dt, kind="ExternalOutput").ap()
            DBG_OUTS.append(("dbg_" + name, list(shape), dt))
            dbg_n[0] += 1
            P.dma("sp", lambda e, ap=ap, d=d: e.dma_start(out=d, in_=ap), "dbgx%d" % dbg_n[0], reads=reads, writes=["dbgx%d" % dbg_n[0]])
        sb = lambda name, shape, dt: stack.enter_context(nc.sbuf_tensor(name, shape, dt))
        NSLAB = 3
        wring = [sb("wring%d" % i, [128, 16, 512], BF16) for i in range(NSLAB)]
        yT = sb("yT", [128, 16, T], BF16)
        cb = sb("cb", [128, NCB], BF16)
        cf = sb("cf", [128, NCF], F32)
        pc = sb("pc", [128, NPC], F32)
        pk = [sb("pk%d" % l, [128, NPK], F32) for l in range(DEPTH)]
        poolw = [sb("poolw%d" % l, [128, 4, 128], BF16) for l in range(DEPTH)]
        xn = sb("xn", [128, D], BF16)
        sm = sb("sm", [128, 64], F32)
        banks = [stack.enter_context(nc.psum_tensor("bank%d" % i, [128, 512], F32)) for i in range(8)]
        bank_rr = [0]

        def nb():
            b = bank_rr[0]
            bank_rr[0] = (b + 1) % 8
            return b

        ident = cb[:, CB_ID:CB_ID + 128]
        ones_b = cb[:, CB_ONE:CB_ONE + 128]
        ml_b = cb[:, CB_ML:CB_ML + 128]
        mu_b = cb[:, CB_MU:CB_MU + 128]
        mneg_b = cb[:, CB_MN:CB_MN + 128]
        cmask = cb[:, CB_CM:CB_CM + 1024]
        evmask = cb[:, CB_EV:CB_EV + 1024]
        tri_f = cf[:, CF_TRI:CF_TRI + 128]
        ones_f = cf[:, CF_ONE:CF_ONE + 128]
        role = pc[:, PC_ROLE:PC_ROLE + 1]
        negrole = pc[:, PC_NEGR:PC_NEGR + 1]

        P.dma("pool", lambda e: e.dma_start(out=cb[:, :], in_=cb_d[:, :]), "c_cb", writes=["cb"])
        P.dma("sp", lambda e: e.dma_start(out=cf[:, :], in_=cf_d[:, :]), "c_cf", writes=["cf"])
        P.dma("sp", lambda e: e.dma_start(out=pc[:, :], in_=pc_d[:, :]), "c_pc", writes=["pc"])
        for l in range(DEPTH):
            P.dma("sp", lambda e, l=l: e.dma_start(out=pk[l][:, :], in_=pk_d[l, :, :]), "c_pk%d" % l, writes=["pk%d" % l])
            P.dma("pool", lambda e, l=l: e.dma_start(out=poolw[l][:, :, :].rearrange("p a b -> p (a b)"), in_=poolw_d[l, :, :]),
                  "c_pw%d" % l, writes=["poolw%d" % l])

        wstate = {"n": 0}

        def load_slab(pieces):
            slot = wstate["n"] % NSLAB
            wstate["n"] += 1
            slab = wring[slot]
            key = "w%d" % slot
            tok = None
            first = wstate["n"] == 1
            for i, (vf, src) in enumerate(pieces):
                tok = P.dma("pool", lambda e, vf=vf, src=src, slab=slab: e.dma_start(out=vf(slab), in_=src),
                            "wsem%d" % slot, writes=[key] if i == 0 else [], reads=["hin1"] if (first and i == 0) else [])
            P.lastw[key] = tok
            return slab, [key]

        def fm_group(uT, bank, slab, wkeys, m0, msz, hf, ukey="uT"):
            def fn(e):
                ins = None
                for kc in range(16):
                    ins = e.matmul(banks[bank][0:msz, 0:512], lhsT=slab[:, kc, m0:m0 + msz],
                                   rhs=uT[:, kc, hf * 512:(hf + 1) * 512], start=(kc == 0), stop=(kc == 15))
                return ins
            return P.op("pe", fn, reads=list(wkeys) + [ukey], writes=["ps%d" % bank])

        def tm_group(uT, bank, slab, wkeys, n0, nsz, i, c0=0, ukey="uT"):
            def fn(e):
                ins = None
                for kc in range(16):
                    ins = e.matmul(banks[bank][:, c0:c0 + nsz], lhsT=uT[:, kc, i * 128:(i + 1) * 128],
                                   rhs=slab[:, kc, n0:n0 + nsz], start=(kc == 0), stop=(kc == 15))
                return ins
            return P.op("pe", fn, reads=list(wkeys) + [ukey], writes=["ps%d" % bank])

        def rmsnorm_to_uT(uT, l, gcol, get_tile, xn2, xn2keys):
            bufs = [(xn[:, :], ["xn"]), (xn2, list(xn2keys))]
            for i in range(NT):
                src, skey = get_tile(i)
                xb, xk = bufs[i % 2]
                P.op("act", lambda e, src=src, i=i, xb=xb: e.activation(out=xb, in_=src, func=AF.Square, accum_out=sm[:, i:i + 1]),
                     reads=[skey], writes=xk + ["sm_ss%d" % i])
                P.op("act", lambda e, i=i: e.activation(out=sm[:, 8 + i:9 + i], in_=sm[:, i:i + 1], func=AF.Sqrt, scale=1.0 / D, bias=eps_col),
                     reads=["sm_ss%d" % i, "epsc"], writes=["sm_rt%d" % i])
                P.op("dve", lambda e, i=i: e.reciprocal(out=sm[:, 16 + i:17 + i], in_=sm[:, 8 + i:9 + i]),
                     reads=["sm_rt%d" % i], writes=["sm_rs%d" % i])
                if i % 2 == 0:
                    P.op("act", lambda e, src=src, i=i, xb=xb: e.activation(out=xb, in_=src, func=AF.Copy, scale=sm[:, 16 + i:17 + i]),
                         reads=[skey, "sm_rs%d" % i], writes=xk)
                else:
                    P.op("dve", lambda e, src=src, i=i, xb=xb: e.tensor_scalar(out=xb, in0=src, scalar1=sm[:, 16 + i:17 + i], scalar2=None, op0=ALU.mult),
                         reads=[skey, "sm_rs%d" % i], writes=xk)
                for half in range(2):
                    b = nb()

                    def fn(e, b=b, half=half, xb=xb):
                        ins = None
                        pv = banks[b][:, :].bitcast(BF16)
                        for cc in range(8):
                            c = half * 8 + cc
                            ins = e.transpose(out=pv[:, cc * 128:(cc + 1) * 128], in_=xb[:, c * 128:(c + 1) * 128], identity=ident)
                        return ins
                    P.op("pe", fn, reads=xk + ["cb"], writes=["ps%d" % b])
                    P.op("dve", lambda e, b=b, half=half, i=i: e.tensor_tensor(
                        out=uT[:, half * 8:(half + 1) * 8, i * 128:(i + 1) * 128],
                        in0=banks[b][:, :].bitcast(BF16).rearrange("p (a t) -> p a t", a=8),
                        in1=pk[l][:, gcol + half * 8:gcol + (half + 1) * 8].unsqueeze(2).to_broadcast([128, 8, 128]),
                        op=ALU.mult), reads=["ps%d" % b, "pk%d" % l], writes=["uT"])

        eps_col = sm[:, 60:61]
        P.op("dve", lambda e: e.memset(eps_col, EPS), writes=["epsc"])
        one_col = sm[:, 61:62]
        P.op("dve", lambda e: e.memset(one_col, 1.0), writes=["onec"])

        def emit_layer(l):
            src_d = x_d if l == 0 else hs_d
            dst_d = hs_d if l < DRUN - 1 else out_d
            srckey = "x" if l == 0 else "hs"
            pkl = pk[l]
            pkk = "pk%d" % l
            win = w_in_d[l].rearrange("(c p) n -> p c n", p=128)
            with contextlib.ExitStack() as mstack:
                msb = lambda name, shape, dt: mstack.enter_context(nc.sbuf_tensor("%s_l%d" % (name, l), shape, dt))
                gqT = msb("gqT", [128, 2, T], BF16)
                gkT = msb("gkT", [128, 2, T], BF16)
                aT = msb("aT", [16, T], F32)
                gv_tm = msb("gv_tm", [128, NT, 512], BF16)
                sgg = msb("sgg", [128, NT, 512], BF16)
                qT = msb("qT", [128, 4, T], BF16)
                nlf = msb("nlf", [128, 32], F32)
                fixA_cb = msb("fixA_cb", [128, 4, 2], F32)
                ufix = msb("ufix", [128, 4, 4], F32)
                pfix = msb("pfix", [128, 4, 32], F32)
                snd = msb("snd", [128, NXF], F32)
                su = mstack.enter_context(contextlib.ExitStack())
                uT = su.enter_context(nc.sbuf_tensor("uTm_l%d" % l, [128, 16, T], BF16))
                with contextlib.ExitStack() as s1:
                    hin = [s1.enter_context(nc.sbuf_tensor("hin%d_l%d" % (i, l), [128, D], F32)) for i in range(4)]

                    def get_tile(i, hin=hin):
                        buf = hin[i % 4]
                        P.dma("sp", lambda e, buf=buf, i=i: e.dma_start(out=buf[:, :], in_=src_d[i * 128:(i + 1) * 128, :]),
                              "hin%d" % (i % 4), reads=["%s%d" % (srckey, i)], writes=["hin%d" % (i % 4)])
                        return buf[:, :], "hin%d" % (i % 4)
                    xnB = s1.enter_context(nc.sbuf_tensor("xnB_l%d" % l, [128, D], BF16))
                    rmsnorm_to_uT(uT, l, PK_GM, get_tile, xnB[:, :], ["xnB"])
                if KDBG and l == 0:
                    P.dma("sp", lambda e, uT=uT: e.dma_start(out=dbg_uT[:, :], in_=uT[:, :, :].rearrange("p a t -> p (a t)")), "dbg1", reads=["uT"], writes=["dbg_uT"])
                if KSTOP == 'n1':
                    return 'stop'
                P.fence()
                with contextlib.ExitStack() as s2:
                    ssb = lambda name, shape, dt: s2.enter_context(nc.sbuf_tensor("%s_l%d" % (name, l), shape, dt))
                    t_cc = ssb("t_cc", [128, 512], F32)
                    uA = ssb("uA", [128, T + 2], F32)
                    acc = ssb("acc", [128, 512], F32)
                    pB = ssb("pB", [128, T + 16], F32)
                    sA = ssb("sA", [128, T + 16], F32)
                    sB = ssb("sB", [128, T + 16], F32)
                    dT = ssb("dT", [128, T], BF16)
                    sq = ssb("sq", [128, 512], BF16)
                    qraw = ssb("qraw", [128, 512], F32)
                    rt2 = [ssb("rt%d" % i, [128, 512], F32) for i in range(2)]
                    qraw2 = [qraw, ssb("qraw1", [128, 512], F32)]
                    kst = [ssb("kst%d" % i, [128, T], BF16) for i in range(2)]
                    vst = [ssb("vst%d" % i, [128, 512], BF16) for i in range(2)]
                    xf = ssb("xf", [128, 32], F32)

                    P.op("dve", lambda e: e.memset(uA[:, 0:2], 0.0), writes=["uA"])
                    P.op("dve", lambda e: e.memset(pB[:, 0:16], 0.0), writes=["pB"])
                    P.op("dve", lambda e: e.memset(sA[:, 0:16], 0.0), writes=["sA"])
                    P.op("dve", lambda e: e.memset(sB[:, 0:16], 0.0), writes=["sB"])
                    for j in range(4):
                        slab, wk = load_slab([
                            (lambda s, g=g: s[:, :, g * 128:(g + 1) * 128], win[:, :, g * 512 + j * 128: g * 512 + (j + 1) * 128])
                            for g in range(3)])
                        for hf in range(2):
                            b0, b1, b2 = nb(), nb(), nb()
                            fm_group(uT, b0, slab, wk, 0, 128, hf)
                            fm_group(uT, b1, slab, wk, 128, 128, hf)
                            fm_group(uT, b2, slab, wk, 256, 128, hf)
                            c0 = hf * 512
                            P.op("act", lambda e, b1=b1: e.activation(out=t_cc[:, :], in_=banks[b1][:, :], func=AF.Copy),
                                 reads=["ps%d" % b1], writes=["t_cc"])
                            P.op("dve", lambda e, b2=b2, c0=c0: e.tensor_tensor(out=uA[:, 2 + c0:2 + c0 + 512], in0=t_cc[:, :], in1=banks[b2][:, :], op=ALU.mult),
                                 reads=["t_cc", "ps%d" % b2], writes=["uA"])
                            P.op("dve", lambda e, c0=c0, j=j: e.tensor_scalar(out=acc[:, :], in0=uA[:, 2 + c0:2 + c0 + 512],
                                 scalar1=pkl[:, PK_CW + j * 3 + 2:PK_CW + j * 3 + 3], scalar2=None, op0=ALU.mult),
                                 reads=["uA", pkk], writes=["acc"])
                            P.op("dve", lambda e, c0=c0, j=j: e.scalar_tensor_tensor(out=acc[:, :], in0=uA[:, 1 + c0:1 + c0 + 512],
                                 scalar=pkl[:, PK_CW + j * 3 + 1:PK_CW + j * 3 + 2], in1=acc[:, :], op0=ALU.mult, op1=ALU.add),
                                 reads=["uA", pkk, "acc"], writes=["acc"])
                            P.op("dve", lambda e, c0=c0, j=j: e.scalar_tensor_tensor(out=acc[:, :], in0=uA[:, c0:c0 + 512],
                                 scalar=pkl[:, PK_CW + j * 3:PK_CW + j * 3 + 1], in1=acc[:, :], op0=ALU.mult, op1=ALU.add),
                                 reads=["uA", pkk, "acc"], writes=["acc"])
                            P.op("dve", lambda e, b0=b0, c0=c0, j=j: e.tensor_tensor(out=yT[:, j, c0:c0 + 512], in0=acc[:, :], in1=banks[b0][:, :], op=ALU.mult),
                                 reads=["acc", "ps%d" % b0], writes=["yT%d" % j])
                            if hf == 0:
                                P.op("act", lambda e, b0=b0, j=j: e.activation(out=fixA_cb[:, j, :], in_=banks[b0][:, 0:2], func=AF.Copy),
                                     reads=["ps%d" % b0], writes=["fixA_cb"])
                                P.op("act", lambda e, j=j: e.activation(out=ufix[:, j, 2:4], in_=uA[:, 2:4], func=AF.Copy),
                                     reads=["uA"], writes=["ufix"])
                            else:
                                P.op("act", lambda e, j=j: e.activation(out=snd[:, XF_A + j * 2:XF_A + j * 2 + 2], in_=uA[:, T:T + 2], func=AF.Copy),
                                     reads=["uA"], writes=["snd_a%d" % j])
                    if KSTOP == 'ipA':
                        return 'stop'
                    slab, wk = load_slab([(lambda s: s[:, :, :], win[:, :, 1536:2048])])
                    qslab, qwk = load_slab([(lambda s: s[:, :, :], win[:, :, 2048:2560])])

                    def gqk_groups(m):
                        dst = gqT if m < 2 else gkT
                        dkey = "gqT" if m < 2 else "gkT"
                        for hf in range(2):
                            b = nb()
                            fm_group(uT, b, qslab, qwk, m * 128, 128, hf)
                            P.op("act", lambda e, b=b, dst=dst, m=m, hf=hf: e.activation(out=dst[:, m % 2, hf * 512:(hf + 1) * 512], in_=banks[b][:, :], func=AF.Copy),
                                 reads=["ps%d" % b], writes=[dkey])
                    for j in range(4):
                        for hf in range(2):
                            b = nb()
                            fm_group(uT, b, slab, wk, j * 128, 128, hf)
                            P.op("act", lambda e, b=b, hf=hf: e.activation(out=pB[:, 16 + hf * 512:16 + (hf + 1) * 512], in_=banks[b][:, :], func=AF.Copy),
                                 reads=["ps%d" % b], writes=["pB"])
                        cur, curk = pB, "pB"
                        pp = [(sA, "sA"), (sB, "sB")]
                        for st in range(j + 1):
                            sh = 1 << st
                            dst, dstk = pp[st % 2]
                            P.op("dve", lambda e, cur=cur, dst=dst, sh=sh: e.tensor_tensor(out=dst[:, 16:16 + T], in0=cur[:, 16:16 + T],
                                 in1=cur[:, 16 - sh:16 - sh + T], op=ALU.add), reads=[curk], writes=[dstk])
                            cur, curk = dst, dstk
                        w = 2 << j
                        P.op("dve", lambda e, cur=cur, w=w: e.scalar_tensor_tensor(out=dT[:, :], in0=cur[:, 16:16 + T], scalar=1.0 / w,
                             in1=pB[:, 16:16 + T], op0=ALU.mult, op1=ALU.subtract), reads=[curk, "pB"], writes=["dT"])
                        P.op("act", lambda e, j=j: e.activation(out=pfix[:, j, 16:32], in_=pB[:, 16:32], func=AF.Copy), reads=["pB"], writes=["pfix"])
                        P.op("act", lambda e, j=j: e.activation(out=snd[:, XF_B + j * 16:XF_B + (j + 1) * 16], in_=pB[:, T:T + 16], func=AF.Copy),
                             reads=["pB"], writes=["snd_b%d" % j])
                        gqk_groups(j)
                        for hf in range(2):
                            b = nb()
                            P.op("pe", lambda e, b=b, j=j, hf=hf: e.matmul(banks[b][:, :], lhsT=poolw[l][:, j, :], rhs=dT[:, hf * 512:(hf + 1) * 512],
                                 start=True, stop=True), reads=["dT", "poolw%d" % l], writes=["ps%d" % b])
                            P.op("act", lambda e, b=b, j=j, hf=hf: e.activation(out=yT[:, 4 + j, hf * 512:(hf + 1) * 512], in_=banks[b][:, :],
                                 func=AF.Copy, scale=pkl[:, PK_PSC + j:PK_PSC + j + 1]), reads=["ps%d" % b, pkk], writes=["yT%d" % (4 + j)])
                    if KSTOP == 'ipB':
                        return 'stop'
                    if KSTOP == 'ipQK':
                        return 'stop'
                    slab, wk = load_slab([(lambda s: s[:, :, :], win[:, :, 2560:3072])])
                    for i in range(NT):
                        b = nb()
                        tm_group(uT, b, slab, wk, 0, 512, i)
                        P.op("act", lambda e, b=b, i=i: e.activation(out=gv_tm[:, i, :], in_=banks[b][:, :], func=AF.Copy),
                             reads=["ps%d" % b], writes=["gv_tm"])
                    slab, wk = load_slab([(lambda s: s[:, :, :], win[:, :, 3072:3584])])
                    for i in range(NT):
                        b = nb()
                        tm_group(uT, b, slab, wk, 0, 512, i)
                        P.op("act", lambda e, b=b: e.activation(out=qraw[:, :], in_=banks[b][:, :], func=AF.Silu),
                             reads=["ps%d" % b], writes=["qraw0"])
                        P.op("dve", lambda e, i=i: e.tensor_tensor(out=sgg[:, i, :].rearrange("p (h v) -> p h v", h=4),
                             in0=qraw[:, :].rearrange("p (h v) -> p h v", h=4),
                             in1=pkl[:, PK_GOG:PK_GOG + 128].unsqueeze(1).to_broadcast([128, 4, 128]), op=ALU.mult),
                             reads=["qraw0", pkk], writes=["sgg"])
                    if KSTOP == 'ipV':
                        return 'stop'
                    slab, wk = load_slab([(lambda s: s[:, :, 0:128], win[:, :, 3472:3600]), (lambda s: s[:, :, 128:256], win[:, :, 5012:5140])])
                    for hf in range(2):
                        b = nb()
                        fm_group(uT, b, slab, wk, 112, 16, hf)
                        P.op("act", lambda e, b=b, hf=hf: e.activation(out=aT[:, hf * 512:(hf + 1) * 512], in_=banks[b][0:16, :], func=AF.Copy),
                             reads=["ps%d" % b], writes=["aT"])
                    b = nb()
                    for i in range(NT):
                        tm_group(uT, b, slab, wk, 240, 16, i, c0=i * 16)
                    P.op("dve", lambda e, b=b: e.tensor_copy(out=xf[:, :].rearrange("p (i h) -> p i h", h=4),
                         in_=banks[b][:, 0:128].rearrange("p (i c) -> p i c", c=16)[:, :, 12:16]), reads=["ps%d" % b], writes=["xf"])
                    P.op("dve", lambda e: e.tensor_tensor(out=xf[:, :], in0=xf[:, :], in1=pkl[:, PK_BF:PK_BF + 32], op=ALU.add),
                         reads=["xf", pkk], writes=["xf"])
                    P.op("act", lambda e: e.activation(out=xf[:, :], in_=xf[:, :], func=AF.Exp, scale=-1.0), reads=["xf"], writes=["xf"])
                    P.op("act", lambda e: e.activation(out=nlf[:, :], in_=xf[:, :], func=AF.Ln, bias=one_col), reads=["xf", "onec"], writes=["nlf"])
                    P.op("dve", lambda e: e.tensor_copy(out=snd[:, XF_NLF:XF_NLF + 32], in_=nlf[:, :]), reads=["nlf"], writes=["snd_nlf"])
                    sndkeys1 = ["snd_a%d" % j for j in range(4)] + ["snd_b%d" % j for j in range(4)] + ["snd_nlf"]
                    P.dma("sp", lambda e: e.dma_start(out=xs_f1[:, :], in_=snd[:, 0:XF_S]), "snd1", reads=sndkeys1, writes=["xs_f1"])
                    P.dma("pool", lambda e: e.collective_compute("AllGather", ALU.bypass, replica_groups=[[2 * i, 2 * i + 1] for i in range(NRUN // 2)],
                          ins=[xs_f1[:, :]], outs=[xr_f1[:, :]]), "cc_f1", reads=["xs_f1"], writes=["xr_f1"], inc=1)
                    for which in range(2):
                        c0w = 3600 + which * 512
                        slab, wk = load_slab([(lambda s: s[:, :, :], win[:, :, c0w:c0w + 512])])
                        gcolq = PK_QG + which
                        items = [(m, hf) for m in range(4) for hf in range(2)]
                        fb = {0: nb()}
                        fm_group(uT, fb[0], slab, wk, 0, 128, 0)
                        for k, (m, hf) in enumerate(items):
                            b = fb[k]
                            qr, qrk = qraw2[k % 2], "qraw%d" % (k % 2)
                            rtt, rtk = rt2[k % 2], "rt%d" % (k % 2)
                            P.op("act", lambda e, b=b: e.activation(out=sq[:, :], in_=banks[b][:, :], func=AF.Square),
                                 reads=["ps%d" % b], writes=["sq"])
                            P.op("act", lambda e, b=b, qr=qr: e.activation(out=qr[:, :], in_=banks[b][:, :], func=AF.Copy),
                                 reads=["ps%d" % b], writes=[qrk])
                            if k + 1 < len(items):
                                fb[k + 1] = nb()
                                fm_group(uT, fb[k + 1], slab, wk, items[k + 1][0] * 128, 128, items[k + 1][1])
                            b2 = nb()
                            P.op("pe", lambda e, b2=b2: e.matmul(banks[b2][:, :], lhsT=ones_b, rhs=sq[:, :], start=True, stop=True),
                                 reads=["sq", "cb"], writes=["ps%d" % b2])
                            P.op("act", lambda e, b2=b2, rtt=rtt: e.activation(out=rtt[:, :], in_=banks[b2][:, :], func=AF.Ln, scale=1.0 / 128, bias=eps_col),
                                 reads=["ps%d" % b2, "epsc"], writes=[rtk])
                            P.op("act", lambda e, rtt=rtt: e.activation(out=rtt[:, :], in_=rtt[:, :], func=AF.Exp, scale=-0.5), reads=[rtk], writes=[rtk])
                            if which == 0:
                                P.op("dve", lambda e, m=m, hf=hf, gcolq=gcolq, qr=qr, rtt=rtt: e.scalar_tensor_tensor(out=qT[:, m, hf * 512:(hf + 1) * 512], in0=qr[:, :],
                                     scalar=pkl[:, gcolq:gcolq + 1], in1=rtt[:, :], op0=ALU.mult, op1=ALU.mult),
                                     reads=[qrk, rtk, pkk], writes=["qT"])
                            else:
                                ks = kst[m % 2]
                                P.op("dve", lambda e, ks=ks, hf=hf, gcolq=gcolq, qr=qr, rtt=rtt: e.scalar_tensor_tensor(out=ks[:, hf * 512:(hf + 1) * 512], in0=qr[:, :],
                                     scalar=pkl[:, gcolq:gcolq + 1], in1=rtt[:, :], op0=ALU.mult, op1=ALU.mult),
                                     reads=[qrk, rtk, pkk], writes=["kst%d" % (m % 2)])
                                if hf == 1:
                                    P.dma("sp", lambda e, ks=ks, m=m: e.dma_start(out=xs_kv[:, m * 1024:(m + 1) * 1024], in_=ks[:, :]),
                                          "kst%d" % (m % 2), reads=["kst%d" % (m % 2)], writes=["xs_k%d" % m])
                    if KSTOP == 'ipF':
                        return 'stop'
                    slab, wk = load_slab([(lambda s: s[:, :, :], win[:, :, 4624:5136])])
                    for i in range(NT):
                        b = nb()
                        tm_group(uT, b, slab, wk, 0, 512, i)
                        vs = vst[i % 2]
                        P.op("act", lambda e, b=b, vs=vs: e.activation(out=vs[:, :], in_=banks[b][:, :], func=AF.Copy),
                             reads=["ps%d" % b], writes=["vst%d" % (i % 2)])
                        P.dma("sp", lambda e, vs=vs, i=i: e.dma_start(
                            out=xs_kv[:, 4096:8192].rearrange("p (h i d) -> p h i d", h=4, i=8)[:, :, i, :],
                            in_=vs[:, :].rearrange("p (h d) -> p h d", h=4)),
                            "vst%d" % (i % 2), reads=["vst%d" % (i % 2)], writes=["xs_v%d" % i])
                    if KSTOP == 'ipFV':
                        return 'stop'
                    P.dma("pool", lambda e: e.collective_compute("AllGather", ALU.bypass, replica_groups=[[2 * i, 2 * i + 1] for i in range(NRUN // 2)],
                          ins=[xs_kv[:, :]], outs=[xr_kv[:, :]]), "cc_kv", reads=["xs_k%d" % m for m in range(4)] + ["xs_v%d" % i for i in range(NT)],
                          writes=["xr_kv"], inc=1)
                if l == 0:
                    dbg("gqT", gqT[:, :, :].rearrange("p a t -> p (a t)"), [128, 2 * T], BF16, ["gqT"])
                    dbg("gkT", gkT[:, :, :].rearrange("p a t -> p (a t)"), [128, 2 * T], BF16, ["gkT"])
                    dbg("aT", aT[:, :], [16, T], F32, ["aT"])
                    dbg("gv", gv_tm[:, :, :].rearrange("p a t -> p (a t)"), [128, NT * 512], BF16, ["gv_tm"])
                    dbg("sgg", sgg[:, :, :].rearrange("p a t -> p (a t)"), [128, NT * 512], BF16, ["sgg"])
                P.fence()
                su.close()
                if KSTOP == 'ip':
                    return 'stop'
                with contextlib.ExitStack() as s3:
                    gsb = lambda name, shape, dt: s3.enter_context(nc.sbuf_tensor("%s_l%d" % (name, l), shape, dt))
                    qd0 = gsb("qd0", [128, 2, T], BF16)
                    qd1 = gsb("qd1", [128, 2, T], BF16)
                    qG = gsb("qG", [128, 2, T], BF16)
                    ST = gsb("ST", [128, NT, 4, 128], BF16)
                    S_bf = gsb("S_bf", [128, 16, 2, 128], BF16)
                    S_f2 = gsb("S_f", [128, 2, 2, 128], F32)
                    elast = gsb("elast", [128, 2, 16], F32)
                    negbd = gsb("negbd", [128, 2], F32)
                    P.op("dve", lambda e: e.tensor_scalar(out=negbd[:, :], in0=pkl[:, PK_BD:PK_BD + 2], scalar1=-1.0, scalar2=None, op0=ALU.mult),
                         reads=[pkk], writes=["negbd"])
                    P.op("dve", lambda e: e.memset(S_f2[:, :, :, :], 0.0), writes=["S_f0", "S_f1"])
                    with contextlib.ExitStack() as s4:
                        tsb = lambda name, shape, dt: s4.enter_context(nc.sbuf_tensor("%s_l%d" % (name, l), shape, dt))
                        sp_ = tsb("sp_", [128, T], F32)
                        ccn = tsb("ccn", [128, T], F32)
                        Gn = tsb("Gn", [128, T], F32)
                        ex = tsb("ex", [128, T], F32)
                        qd = tsb("qd", [128, T], BF16)
                        kd = tsb("kd", [128, T], BF16)
                        qu = tsb("qu", [128, T], BF16)
                        ku = tsb("ku", [128, T], BF16)
                        klT = tsb("klT", [128, T], BF16)
                        kl_tm = tsb("kl_tm", [128, NT, 128], BF16)
                        t1 = tsb("t1", [128, 512], F32)
                        t2 = tsb("t2", [128, 512], F32)
                        wdec = pkl[0:16, PK_WD:PK_WD + 256]
                        for c in range(2):
                            for hf in range(2):
                                b = nb()
                                P.op("pe", lambda e, b=b, c=c, hf=hf: e.matmul(banks[b][:, :], lhsT=wdec[:, c * 128:(c + 1) * 128],
                                     rhs=aT[:, hf * 512:(hf + 1) * 512], start=True, stop=True), reads=[pkk, "aT"], writes=["ps%d" % b])
                                P.op("act", lambda e, b=b, c=c, hf=hf: e.activation(out=ex[:, hf * 512:(hf + 1) * 512], in_=banks[b][:, :], func=AF.Exp,
                                     scale=-1.0, bias=negbd[:, c:c + 1]), reads=["ps%d" % b, "negbd"], writes=["ex"])
                            P.op("act", lambda e: e.activation(out=sp_[:, :], in_=ex[:, :], func=AF.Ln, bias=one_col), reads=["ex", "onec"], writes=["sp_"])
                            P.op("dve", lambda e: e.tensor_tensor_scan(out=ccn[:, :], data0=cmask, data1=sp_[:, :], initial=0.0, op0=ALU.mult, op1=ALU.add),
                                 reads=["sp_", "cb"], writes=["ccn"])
                            P.op("dve", lambda e: e.tensor_tensor_scan(out=Gn[:, :], data0=one_col.to_broadcast([128, T]), data1=sp_[:, :], initial=0.0,
                                 op0=ALU.mult, op1=ALU.add), reads=["sp_", "onec"], writes=["Gn"])
                            if KSTOP == 'g1':
                                return 'stop'
                            lastn = ccn[:, :].rearrange("p (n s) -> p n s", s=64)[:, :, 63:64]
                            P.op("act", lambda e: e.activation(out=ex[:, :], in_=ccn[:, :], func=AF.Exp, scale=-1.0 / 16), reads=["ccn"], writes=["ex"])
                            P.op("act", lambda e: e.activation(out=sp_[:, :], in_=ccn[:, :], func=AF.Exp, scale=1.0 / 16), reads=["ccn"], writes=["sp_"])
                            P.op("act", lambda e: e.activation(out=Gn[:, :], in_=Gn[:, :], func=AF.Exp, scale=-1.0 / 16), reads=["Gn"], writes=["Gn"])
                            P.op("act", lambda e, c=c, lastn=lastn: e.activation(out=elast[:, c, :], in_=lastn.rearrange("p n s -> p (n s)"), func=AF.Exp, scale=-1.0 / 16),
                                 reads=["ccn"], writes=["elast"])
                            P.op("dve", lambda e, c=c: e.scalar_tensor_tensor(out=qd[:, :], in0=gqT[:, c, :], scalar=0.125, in1=ex[:, :], op0=ALU.mult, op1=ALU.mult),
                                 reads=["gqT", "ex"], writes=["qd"])
                            P.op("dve", lambda e, c=c: e.tensor_tensor(out=ku[:, :], in0=gkT[:, c, :], in1=ex[:, :], op=ALU.mult),
                                 reads=["gkT", "ex"], writes=["ku"])
                            P.op("dve", lambda e, lastn=lastn: e.tensor_tensor(out=ex[:, :].rearrange("p (n s) -> p n s", s=64),
                                 in0=ccn[:, :].rearrange("p (n s) -> p n s", s=64), in1=lastn.to_broadcast([128, 16, 64]), op=ALU.subtract),
                                 reads=["ccn"], writes=["ex"])
                            P.op("act", lambda e: e.activation(out=ex[:, :], in_=ex[:, :], func=AF.Exp, scale=1.0 / 16), reads=["ex"], writes=["ex"])
                            P.op("pool", lambda e, c=c: e.tensor_tensor(out=qd0[:, c, :], in0=qd[:, :], in1=evmask, op=ALU.mult),
                                 reads=["qd", "cb"], writes=["qd0"])
                            P.op("pool", lambda e, c=c: e.tensor_tensor(out=qd1[:, c, :], in0=qd[:, :], in1=qd0[:, c, :], op=ALU.subtract),
                                 reads=["qd", "qd0"], writes=["qd1"])
                            P.op("dve", lambda e, c=c: e.scalar_tensor_tensor(out=qu[:, :], in0=gqT[:, c, :], scalar=0.125, in1=sp_[:, :], op0=ALU.mult, op1=ALU.mult),
                                 reads=["gqT", "sp_"], writes=["qu"])
                            P.op("dve", lambda e, c=c: e.tensor_tensor(out=kd[:, :], in0=gkT[:, c, :], in1=sp_[:, :], op=ALU.mult),
                                 reads=["gkT", "sp_"], writes=["kd"])
                            P.op("dve", lambda e, c=c: e.scalar_tensor_tensor(out=qG[:, c, :], in0=gqT[:, c, :], scalar=0.125, in1=Gn[:, :], op0=ALU.mult, op1=ALU.mult),
                                 reads=["gqT", "Gn"], writes=["qG"])
                            P.op("dve", lambda e, c=c: e.tensor_tensor(out=klT[:, :], in0=gkT[:, c, :], in1=ex[:, :], op=ALU.mult),
                                 reads=["gkT", "ex"], writes=["klT"])
                            if KSTOP == 'g2':
                                return 'stop'
                            b = nb()

                            def fn(e, b=b):
                                ins = None
                                pv = banks[b][:, :].bitcast(BF16)
                                for i in range(NT):
                                    ins = e.transpose(out=pv[:, i * 128:(i + 1) * 128], in_=klT[:, i * 128:(i + 1) * 128], identity=ident)
                                return ins
                            P.op("pe", fn, reads=["klT", "cb"], writes=["ps%d" % b])
                            P.op("dve", lambda e, b=b: e.tensor_copy(out=kl_tm[:, :, :], in_=banks[b][:, :].bitcast(BF16).rearrange("p (i k) -> p i k", i=NT)),
                                 reads=["ps%d" % b], writes=["kl_tm"])
                            if KSTOP == 'g3':
                                return 'stop'
                            for i0 in range(0, NT, 4):
                                for eh in range(2):
                                    ba, bb = nb(), nb()

                                    def fa(e, bk, kk, qq, i0=i0, eh=eh):
                                        ins = None
                                        for ii in range(4):
                                            i = i0 + ii
                                            ins = e.matmul(banks[bk][:, ii * 128:(ii + 1) * 128], lhsT=kk[eh * 64:(eh + 1) * 64, i * 128:(i + 1) * 128],
                                                           rhs=qq[eh * 64:(eh + 1) * 64, i * 128:(i + 1) * 128], start=True, stop=True)
                                        return ins
                                    P.op("pe", lambda e, ba=ba, fa=fa: fa(e, ba, kd, qd), reads=["kd", "qd"], writes=["ps%d" % ba])
                                    P.op("pe", lambda e, bb=bb, fa=fa: fa(e, bb, ku, qu), reads=["ku", "qu"], writes=["ps%d" % bb])
                                    P.op("dve", lambda e, ba=ba: e.tensor_tensor(out=t1[:, :].rearrange("p (a t) -> p a t", a=4),
                                         in0=banks[ba][:, :].rearrange("p (a t) -> p a t", a=4), in1=ml_b.unsqueeze(1).to_broadcast([128, 4, 128]), op=ALU.mult),
                                         reads=["ps%d" % ba, "cb"], writes=["t1"])
                                    P.op("dve", lambda e, bb=bb: e.tensor_tensor(out=t2[:, :].rearrange("p (a t) -> p a t", a=4),
                                         in0=banks[bb][:, :].rearrange("p (a t) -> p a t", a=4), in1=mu_b.unsqueeze(1).to_broadcast([128, 4, 128]), op=ALU.mult),
                                         reads=["ps%d" % bb, "cb"], writes=["t2"])
                                    P.op("dve", lambda e, i0=i0, c=c, eh=eh: e.tensor_tensor(out=ST[:, i0:i0 + 4, 2 * c + eh, :],
                                         in0=t1[:, :].rearrange("p (i t) -> p i t", i=4), in1=t2[:, :].rearrange("p (i t) -> p i t", i=4), op=ALU.add),
                                         reads=["t1", "t2"], writes=["ST"])
                            if KSTOP == 'g4':
                                return 'stop'
                            for n0 in range(0, 16, 4):
                                bev, bod = nb(), nb()

                                def fu(e, n0=n0, c=c, par=0, bk=bev):
                                    ins = None
                                    r0 = par * 64
                                    for nn in range(2):
                                        n = n0 + 2 * nn + par
                                        ins = e.matmul(banks[bk][:, nn * 256:(nn + 1) * 256], lhsT=kl_tm[r0:r0 + 64, n // 2, :],
                                                       rhs=gv_tm[r0:r0 + 64, n // 2, c * 256:(c + 1) * 256], start=True, stop=True)
                                    return ins
                                P.op("pe", lambda e, fu=fu, bev=bev: fu(e, par=0, bk=bev), reads=["kl_tm", "gv_tm"], writes=["ps%d" % bev])
                                P.op("pe", lambda e, fu=fu, bod=bod: fu(e, par=1, bk=bod), reads=["kl_tm", "gv_tm"], writes=["ps%d" % bod])
                                for n in range(n0, n0 + 4):
                                    par = n % 2
                                    bk = bev if par == 0 else bod
                                    nn = (n - n0) // 2
                                    for eh in range(2):
                                        r0 = eh * 64
                                        P.op("dve", lambda e, bk=bk, nn=nn, n=n, eh=eh, r0=r0, c=c: e.scalar_tensor_tensor(
                                            out=S_f2[r0:r0 + 64, n % 2, c, :], in0=S_f2[r0:r0 + 64, (n + 1) % 2, c, :], scalar=elast[r0:r0 + 64, c, n:n + 1],
                                            in1=banks[bk][r0:r0 + 64, nn * 256 + eh * 128:nn * 256 + (eh + 1) * 128], op0=ALU.mult, op1=ALU.add),
                                            reads=["S_f%d" % ((n + 1) % 2), "elast", "ps%d" % bk], writes=["S_f%d" % (n % 2)])
                                    P.op("act", lambda e, n=n, c=c: e.activation(out=S_bf[:, n, c, :], in_=S_f2[:, n % 2, c, :], func=AF.Copy),
                                         reads=["S_f%d" % (n % 2)], writes=["S_bf"])
                            P.op("act", lambda e, c=c: e.activation(out=snd[:, XF_S + c * 128:XF_S + (c + 1) * 128], in_=S_f2[:, 1, c, :], func=AF.Copy),
                                 reads=["S_f1"], writes=["snd_s%d" % c])
                    if l == 0:
                        dbg("snd", snd[:, :], [128, NXF], F32, ["snd_s0", "snd_s1", "snd_nlf"])
                        dbg("ST", ST[:, :, :, :].rearrange("p a b t -> p (a b t)"), [128, NT * 512], BF16, ["ST"])
                        dbg("Sbf", S_bf[:, :, :, :].rearrange("p a b t -> p (a b t)"), [128, 16 * 256], BF16, ["S_bf"])
                        dbg("qd0", qd0[:, :, :].rearrange("p a t -> p (a t)"), [128, 2 * T], BF16, ["qd0"])
                        dbg("qd1", qd1[:, :, :].rearrange("p a t -> p (a t)"), [128, 2 * T], BF16, ["qd1"])
                        dbg("qG", qG[:, :, :].rearrange("p a t -> p (a t)"), [128, 2 * T], BF16, ["qG"])
                    P.fence()
                    if KSTOP == 'gla':
                        return 'stop'
                    P.dma("sp", lambda e: e.dma_start(out=xs_f2[:, :], in_=snd[:, XF_S:XF_S + 256]), "snd2", reads=["snd_s0", "snd_s1"], writes=["xs_f2"])
                    P.dma("pool", lambda e: e.collective_compute("AllGather", ALU.bypass, replica_groups=[[2 * i, 2 * i + 1] for i in range(NRUN // 2)],
                          ins=[xs_f2[:, :]], outs=[xr_f2[:, :]]), "cc_f2", reads=["xs_f2"], writes=["xr_f2"], inc=1)
                    rcv = gsb("rcv", [128, NXF], F32)
                    P.dma("sp", lambda e: e.dma_start(out=rcv[:, 0:XF_S], in_=xr_f1[0:128, :]), "rcv1", reads=["xr_f1"], writes=["rcv1"])
                    S0_bf = gsb("S0_bf", [128, 2, 128], BF16)
                    P.op("dve", lambda e: e.tensor_scalar(out=ufix[:, :, 0:2], in0=rcv[:, XF_A:XF_A + 8].rearrange("p (j t) -> p j t", j=4), scalar1=role,
                         scalar2=None, op0=ALU.mult), reads=["rcv1", "pc"], writes=["ufix"])
                    P.op("dve", lambda e: e.tensor_scalar(out=pfix[:, :, 0:16], in0=rcv[:, XF_B:XF_B + 64].rearrange("p (j t) -> p j t", j=4), scalar1=role,
                         scalar2=None, op0=ALU.mult), reads=["rcv1", "pc"], writes=["pfix"])
                    if KSTOP == 'xchg':
                        return 'stop'
                    fx = gsb("fx", [128, 8], F32)
                    pf1 = gsb("pf1", [128, 32], F32)
                    pf2 = gsb("pf2", [128, 32], F32)
                    dfx = gsb("dfx", [128, 16], BF16)
                    for j in range(4):
                        P.op("dve", lambda e, j=j: e.tensor_scalar(out=fx[:, 0:2], in0=ufix[:, j, 2:4], scalar1=pkl[:, PK_CW + j * 3 + 2:PK_CW + j * 3 + 3],
                             scalar2=None, op0=ALU.mult), reads=["ufix", pkk], writes=["fx"])
                        P.op("dve", lambda e, j=j: e.scalar_tensor_tensor(out=fx[:, 0:2], in0=ufix[:, j, 1:3], scalar=pkl[:, PK_CW + j * 3 + 1:PK_CW + j * 3 + 2],
                             in1=fx[:, 0:2], op0=ALU.mult, op1=ALU.add), reads=["ufix", pkk, "fx"], writes=["fx"])
                        P.op("dve", lambda e, j=j: e.scalar_tensor_tensor(out=fx[:, 0:2], in0=ufix[:, j, 0:2], scalar=pkl[:, PK_CW + j * 3:PK_CW + j * 3 + 1],
                             in1=fx[:, 0:2], op0=ALU.mult, op1=ALU.add), reads=["ufix", pkk, "fx"], writes=["fx"])
                        P.op("dve", lambda e, j=j: e.tensor_tensor(out=yT[:, j, 0:2], in0=fx[:, 0:2], in1=fixA_cb[:, j, :], op=ALU.mult),
                             reads=["fx", "fixA_cb"], writes=["yT%d" % j])
                    for j in range(4):
                        cur = pfix[:, j, :]
                        curk = "pfix"
                        pp = [(pf1, "pf1"), (pf2, "pf2")]
                        lo = 0
                        for st in range(j + 1):
                            sh = 1 << st
                            dst, dstk = pp[st % 2]
                            lo2 = lo + sh
                            P.op("dve", lambda e, cur=cur, dst=dst, sh=sh, lo2=lo2: e.tensor_tensor(out=dst[:, lo2:32], in0=cur[:, lo2:32],
                                 in1=cur[:, lo2 - sh:32 - sh], op=ALU.add), reads=[curk], writes=[dstk])
                            cur, curk, lo = dst[:, :], dstk, lo2
                        mk = "pf1" if curk == "pf2" else "pf2"
                        mt = pf1 if curk == "pf2" else pf2
                        P.op("dve", lambda e, cur=cur, j=j, mt=mt: e.tensor_tensor(out=mt[:, 0:16], in0=cur[:, 16:32],
                             in1=pc[:, PC_INVC + j * 16:PC_INVC + (j + 1) * 16], op=ALU.mult), reads=[curk, "pc"], writes=[mk])
                        P.op("dve", lambda e, mt=mt, j=j: e.tensor_tensor(out=dfx[:, :], in0=mt[:, 0:16], in1=pfix[:, j, 16:32], op=ALU.subtract),
                             reads=[mk, "pfix"], writes=["dfx"])
                        b = nb()
                        P.op("pe", lambda e, b=b, j=j: e.matmul(banks[b][:, 0:16], lhsT=poolw[l][:, j, :], rhs=dfx[:, :], start=True, stop=True),
                             reads=["dfx", "poolw%d" % l], writes=["ps%d" % b])
                        P.op("act", lambda e, b=b, j=j: e.activation(out=yT[:, 4 + j, 0:16], in_=banks[b][:, 0:16], func=AF.Copy,
                             scale=pkl[:, PK_PSC + j:PK_PSC + j + 1]), reads=["ps%d" % b, pkk], writes=["yT%d" % (4 + j)])
                    def emit_s0():
                        P.dma("sp", lambda e: e.dma_start(out=rcv[:, XF_S:XF_S + 256], in_=xr_f2[0:128, :]), "rcv2", reads=["xr_f2"], writes=["rcv2"])
                        P.op("dve", lambda e: e.tensor_scalar(out=S0_bf[:, :, :].rearrange("p c v -> p (c v)"), in0=rcv[:, XF_S:XF_S + 256], scalar1=role,
                             scalar2=None, op0=ALU.mult), reads=["rcv2", "pc"], writes=["S0_bf"])
                    yc2 = [gsb("yc%d" % k, [128, 512], BF16) for k in range(2)]
                    ssg2 = [gsb("ssg%d" % k, [128, 8], F32) for k in range(2)]
                    junk = gsb("junk", [128, 128], BF16)
                    def gla_out_tile(i):
                        b = 6
                        yc, ssg = yc2[i % 2], ssg2[i % 2]
                        yck, ssk = "yc%d" % (i % 2), "ssg%d" % (i % 2)

                        def fo(e, b=b, i=i):
                            ins = None
                            for hh in range(4):
                                c, eh = hh // 2, hh % 2
                                r0 = eh * 64
                                o = banks[b][:, hh * 128:(hh + 1) * 128]
                                tok = slice(i * 128, (i + 1) * 128)
                                e.matmul(o, lhsT=ST[:, i, hh, :], rhs=gv_tm[:, i, hh * 128:(hh + 1) * 128], start=True, stop=False)
                                if i > 0:
                                    e.matmul(o, lhsT=qd0[r0:r0 + 64, c, tok], rhs=S_bf[r0:r0 + 64, 2 * i - 1, c, :], start=False, stop=False)
                                e.matmul(o, lhsT=qd1[r0:r0 + 64, c, tok], rhs=S_bf[r0:r0 + 64, 2 * i, c, :], start=False, stop=False)
                                ins = e.matmul(o, lhsT=qG[r0:r0 + 64, c, tok], rhs=S0_bf[r0:r0 + 64, c, :], start=False, stop=True)
                            return ins
                        P.op("pe", fo, reads=["ST", "gv_tm", "qd0", "qd1", "qG", "S_bf", "S0_bf"], writes=["ps%d" % b])
                        for hh in range(4):
                            P.op("act", lambda e, b=b, hh=hh, ssg=ssg: e.activation(out=junk[:, :], in_=banks[b][:, hh * 128:(hh + 1) * 128], func=AF.Square,
                                 accum_out=ssg[:, hh:hh + 1]), reads=["ps%d" % b], writes=["junk", ssk])
                        P.op("act", lambda e, ssg=ssg: e.activation(out=ssg[:, 4:8], in_=ssg[:, 0:4], func=AF.Sqrt, scale=1.0 / 128, bias=eps_col),
                             reads=[ssk, "epsc"], writes=[ssk])
                        P.op("dve", lambda e, ssg=ssg: e.reciprocal(out=ssg[:, 4:8], in_=ssg[:, 4:8]), reads=[ssk], writes=[ssk])
                        for hh in range(4):
                            P.op("dve", lambda e, b=b, hh=hh, i=i, yc=yc, ssg=ssg: e.scalar_tensor_tensor(out=yc[:, hh * 128:(hh + 1) * 128], in0=banks[b][:, hh * 128:(hh + 1) * 128],
                                 scalar=ssg[:, 4 + hh:5 + hh], in1=sgg[:, i, hh * 128:(hh + 1) * 128], op0=ALU.mult, op1=ALU.mult),
                                 reads=["ps%d" % b, ssk, "sgg"], writes=[yck])

                    def gla_out_tile_b(i):
                        yc = yc2[i % 2]
                        yck = "yc%d" % (i % 2)
                        b2 = 7

                        def ft(e, b2=b2, yc=yc):
                            ins = None
                            pv = banks[b2][:, :].bitcast(BF16)
                            for hh in range(4):
                                ins = e.transpose(out=pv[:, hh * 128:(hh + 1) * 128], in_=yc[:, hh * 128:(hh + 1) * 128], identity=ident)
                            return ins
                        P.op("pe", ft, reads=[yck, "cb"], writes=["ps%d" % b2])
                        P.op("act", lambda e, b2=b2, i=i: e.activation(out=yT[:, 8:12, i * 128:(i + 1) * 128],
                             in_=banks[b2][:, :].bitcast(BF16)[:, 0:512].rearrange("p (h t) -> p h t", h=4), func=AF.Copy),
                             reads=["ps%d" % b2], writes=["yT8", "yT9", "yT10", "yT11"])
                    if KSTOP == 'glao':
                        return 'stop'
                    fsb_ = gsb("fsb", [128, 128], F32)
                    pin = gsb("pin", [128, 64], F32)
                    fbias = gsb("fbias", [128, 2, 64], F32)
                    nlfp = rcv[:, XF_NLF:XF_NLF + 32]
                    b = nb()

                    def ff_(e, b=b):
                        e.matmul(banks[b][:, 0:32], lhsT=tri_f, rhs=nlfp, start=True, stop=True)
                        e.matmul(banks[b][:, 32:64], lhsT=tri_f, rhs=nlf[:, :], start=True, stop=True)
                        e.matmul(banks[b][:, 64:96], lhsT=ones_f, rhs=nlfp, start=True, stop=True)
                        return e.matmul(banks[b][:, 96:128], lhsT=ones_f, rhs=nlf[:, :], start=True, stop=True)
                    P.op("pe", ff_, reads=["cf", "rcv1", "nlf"], writes=["ps%d" % b])
                    P.op("dve", lambda e, b=b: e.tensor_copy(out=fsb_[:, :], in_=banks[b][:, 0:128]), reads=["ps%d" % b], writes=["fsb"])
                    for hh in range(4):
                        P.op("dve", lambda e, hh=hh: e.tensor_tensor_scan(out=pin[:, :].rearrange("p (j h) -> p j h", h=4)[:, :, hh],
                             data0=one_col.to_broadcast([128, 16]), data1=fsb_[:, 64:128].rearrange("p (j h) -> p j h", h=4)[:, :, hh],
                             initial=0.0, op0=ALU.mult, op1=ALU.add), reads=["fsb", "onec"], writes=["pin"])
                    P.op("dve", lambda e: e.tensor_tensor(out=pin[:, :], in0=pin[:, :], in1=fsb_[:, 64:128], op=ALU.subtract), reads=["pin", "fsb"], writes=["pin"])
                    P.op("dve", lambda e: e.tensor_tensor(out=fsb_[:, 0:64], in0=fsb_[:, 0:64], in1=pin[:, :], op=ALU.add), reads=["pin", "fsb"], writes=["fsb"])
                    for g in range(2):
                        jr = 8 + 4 * g + 2
                        P.op("dve", lambda e, g=g, jr=jr: e.tensor_tensor(out=fbias[:, g, :].rearrange("p (j h) -> p j h", h=4),
                             in0=fsb_[:, 0:64].rearrange("p (j h) -> p j h", h=4),
                             in1=pin[:, jr * 4:(jr + 1) * 4].unsqueeze(1).to_broadcast([128, 16, 4]), op=ALU.subtract),
                             reads=["fsb", "pin"], writes=["fbias"])
                        P.op("dve", lambda e, g=g: e.tensor_scalar(out=fbias[:, g, 0:32], in0=fbias[:, g, 0:32], scalar1=negrole, scalar2=None, op0=ALU.add),
                             reads=["fbias", "pc"], writes=["fbias"])
                    kp = [gsb("kp%d" % i, [128, T], BF16) for i in range(2)]
                    ko = [gsb("ko%d" % i, [128, T], BF16) for i in range(2)]
                    vp = [gsb("vp%d" % i, [128, NT, 128], BF16) for i in range(2)]
                    vo = [gsb("vo%d" % i, [128, NT, 128], BF16) for i in range(2)]
                    PT = [gsb("PT%d" % i, [128, 512], BF16) for i in range(5)]
                    rden = gsb("rden", [128, 512], F32)
                    numS = gsb("numS", [128, 512], F32)
                    ptn = 0
                    scn = [0]
                    scale = 128.0 ** -0.5
                    for hh in range(4):
                        s = hh % 2
                        P.dma("sp", lambda e, s=s, hh=hh: e.dma_start(out=kp[s][:, :], in_=xr_kv[0:128, hh * 1024:(hh + 1) * 1024]), "kp%d" % s,
                              reads=["xr_kv"], writes=["kp%d" % s])
                        P.dma("sp", lambda e, s=s, hh=hh: e.dma_start(out=ko[s][:, :], in_=xs_kv[:, hh * 1024:(hh + 1) * 1024]), "ko%d" % s,
                              reads=["xs_k%d" % hh], writes=["ko%d" % s])
                        P.dma("sp", lambda e, s=s, hh=hh: e.dma_start(out=vp[s][:, :, :].rearrange("p i d -> p (i d)"),
                              in_=xr_kv[0:128, 4096 + hh * 1024:4096 + (hh + 1) * 1024]), "vp%d" % s, reads=["xr_kv"], writes=["vp%d" % s])
                        P.dma("sp", lambda e, s=s, hh=hh: e.dma_start(out=vo[s][:, :, :].rearrange("p i d -> p (i d)"),
                              in_=xs_kv[:, 4096 + hh * 1024:4096 + (hh + 1) * 1024]), "vo%d" % s, reads=["xs_v%d" % i for i in range(NT)], writes=["vo%d" % s])
                        for g in range(2):
                            q0 = g * 512
                            bnum, bden = 0, 1
                            nJ = 8 + 4 * g + 4
                            pend = []
                            for J in range(nJ):
                                r = J - (8 + 4 * g)
                                nc0 = max(r, 0) * 128
                                if J < 8:
                                    kblk, kkey = kp[s][:, J * 128:(J + 1) * 128], "kp%d" % s
                                    vblk, vkey = vp[s][:, J, :], "vp%d" % s
                                else:
                                    kblk, kkey = ko[s][:, (J - 8) * 128:(J - 7) * 128], "ko%d" % s
                                    vblk, vkey = vo[s][:, J - 8, :], "vo%d" % s
                                bs = 2 + scn[0] % 4
                                scn[0] += 1

                                def fs(e, bs=bs, kblk=kblk, nc0=nc0, r=r, q0=q0, hh=hh):
                                    ins = e.matmul(banks[bs][:, nc0:512], lhsT=kblk, rhs=qT[:, hh, q0 + nc0:q0 + 512], start=True, stop=(r < 0))
                                    if r >= 0:
                                        ins = e.matmul(banks[bs][:, nc0:nc0 + 128], lhsT=ident, rhs=mneg_b, start=False, stop=True)
                                    return ins
                                P.op("pe", fs, reads=[kkey, "qT", "cb"], writes=["ps%d" % bs])
                                pt = PT[ptn % 5]
                                ptk = "PT%d" % (ptn % 5)
                                ptn += 1
                                P.op("act", lambda e, bs=bs, pt=pt, nc0=nc0, g=g, J=J, hh=hh: e.activation(out=pt[:, nc0:512], in_=banks[bs][:, nc0:512], func=AF.Exp,
                                     scale=scale, bias=fbias[:, g, J * 4 + hh:J * 4 + hh + 1]), reads=["ps%d" % bs, "fbias"], writes=[ptk])

                                def pv(vblk=vblk, vkey=vkey, pt=pt, ptk=ptk, nc0=nc0, J=J, bnum=bnum, bden=bden, nJ=nJ):
                                    P.op("pe", lambda e: e.matmul(banks[bnum][:, nc0:512], lhsT=vblk, rhs=pt[:, nc0:512],
                                         start=(J == 0), stop=(J == nJ - 1)), reads=[vkey, ptk], writes=["ps%d" % bnum])
                                    P.op("pe", lambda e: e.matmul(banks[bden][:, nc0:512], lhsT=ones_b, rhs=pt[:, nc0:512],
                                         start=(J == 0), stop=(J == nJ - 1)), reads=["cb", ptk], writes=["ps%d" % bden])
                                pend.append(pv)
                                if len(pend) > 3:
                                    pend.pop(0)()
                            while pend:
                                pend.pop(0)()
                            P.op("act", lambda e, bden=bden: e.activation(out=rden[:, :], in_=banks[bden][:, :], func=AF.Copy), reads=["ps%d" % bden], writes=["rden"])
                            P.op("act", lambda e, bnum=bnum: e.activation(out=numS[:, :], in_=banks[bnum][:, :], func=AF.Copy), reads=["ps%d" % bnum], writes=["numS"])
                            P.op("dve", lambda e: e.reciprocal(out=rden[:, :], in_=rden[:, :]), reads=["rden"], writes=["rden"])
                            P.op("dve", lambda e, hh=hh, q0=q0: e.tensor_tensor(out=yT[:, 12 + hh, q0:q0 + 512], in0=numS[:, :], in1=rden[:, :], op=ALU.mult),
                                 reads=["numS", "rden"], writes=["yT%d" % (12 + hh)])
                            for kind, ti in GLA_SCHED[hh * 2 + g]:
                                if kind == 'S':
                                    emit_s0()
                                elif kind == 'A':
                                    gla_out_tile(ti)
                                else:
                                    gla_out_tile_b(ti)
                P.fence()
            with contextlib.ExitStack() as fstack:
                h = fstack.enter_context(nc.sbuf_tensor("h_l%d" % l, [128, NT, D], F32))
                xin_t = fstack.enter_context(nc.sbuf_tensor("xin_l%d" % l, [128, 2, 512], F32))
                xin = [xin_t[:, 0, :], xin_t[:, 1, :]]
                sgt = [fstack.enter_context(nc.sbuf_tensor("sgt%d_l%d" % (i, l), [128, 512], F32)) for i in range(2)]
                uT = fstack.enter_context(nc.sbuf_tensor("uTf_l%d" % l, [128, 16, T], BF16))
                wo = w_out_d[l].rearrange("(c p) n -> p c n", p=128)
                ytkeys = ["yT%d" % j for j in range(16)]
                if KDBG and l == 0:
                    P.dma("sp", lambda e: e.dma_start(out=dbg_yT[:, :], in_=yT[:, :, :].rearrange("p a t -> p (a t)")), "dbg2", reads=ytkeys, writes=["dbg_yT"])
                xn_i = 0
                for cg in range(4):
                    slab, wk = load_slab([(lambda s: s[:, :, :], wo[:, :, cg * 512:(cg + 1) * 512])])
                    for i in range(NT):
                        b = nb()

                        def fo2(e, b=b, slab=slab, i=i):
                            ins = None
                            for kc in range(16):
                                ins = e.matmul(banks[b][:, :], lhsT=yT[:, kc, i * 128:(i + 1) * 128], rhs=slab[:, kc, :], start=(kc == 0), stop=(kc == 15))
                            return ins
                        P.op("pe", fo2, reads=wk + ytkeys, writes=["ps%d" % b])
                        xb_ = xin[xn_i % 2]
                        xk = "xin%d" % (xn_i % 2)
                        xn_i += 1
                        P.dma("sp", lambda e, xb_=xb_, i=i, cg=cg: e.dma_start(out=xb_, in_=src_d[i * 128:(i + 1) * 128, cg * 512:(cg + 1) * 512]),
                              xk, reads=["%s%d" % (srckey, i)], writes=[xk])
                        P.op("dve", lambda e, b=b, xb_=xb_, i=i, cg=cg: e.tensor_tensor(out=h[:, i, cg * 512:(cg + 1) * 512], in0=xb_, in1=banks[b][:, :], op=ALU.add),
                             reads=["ps%d" % b, xk], writes=["h%d" % i])
                if KDBG and l == 0:
                    P.dma("sp", lambda e: e.dma_start(out=dbg_h1[:, :], in_=h[:, :, :].rearrange("p a t -> p (a t)")), "dbg3", reads=["h%d" % i for i in range(NT)], writes=["dbg_h1"])
                if KSTOP == 'out':
                    return 'stop'
                rmsnorm_to_uT(uT, l, PK_GF, lambda i: (h[:, i, :], "h%d" % i),
                              xin_t[:, :, :].rearrange("p a b -> p (a b)").bitcast(BF16), ["xin0", "xin1"])
                wg = w_gate_d[l].rearrange("(c p) n -> p c n", p=128)
                wu = w_up_d[l].rearrange("(c p) n -> p c n", p=128)
                wd = w_down_d[l].rearrange("(f p) n -> p f n", p=128)
                actT = yT
                f0 = 0
                sgn = 0
                for nf in (16, 16, 12):
                    for sub in range(0, nf, 2):
                        cs = (f0 + sub) * 128
                        gslab, gk_ = load_slab([(lambda s: s[:, :, 0:256], wg[:, :, cs:cs + 256]), (lambda s: s[:, :, 256:512], wu[:, :, cs:cs + 256])])
                        uslab, uk_ = gslab, gk_
                        for m in range(2):
                            fl = sub + m
                            for hf in range(2):
                                bg, bu = nb(), nb()
                                fm_group(uT, bg, gslab, gk_, m * 128, 128, hf)
                                fm_group(uT, bu, uslab, uk_, 256 + m * 128, 128, hf)
                                sg_ = sgt[sgn % 2]
                                sgk = "sgt%d" % (sgn % 2)
                                sgn += 1
                                P.op("act", lambda e, bg=bg, sg_=sg_: e.activation(out=sg_[:, :], in_=banks[bg][:, :], func=AF.Silu),
                                     reads=["ps%d" % bg], writes=[sgk])
                                P.op("dve", lambda e, bu=bu, sg_=sg_, fl=fl, hf=hf: e.tensor_tensor(out=actT[:, fl, hf * 512:(hf + 1) * 512], in0=sg_[:, :],
                                     in1=banks[bu][:, :], op=ALU.mult), reads=["ps%d" % bu, sgk], writes=["yT%d" % fl])
                    for cg in range(4):
                        slab, wk = load_slab([(lambda s, nf=nf: s[:, 0:nf, :], wd[:, f0:f0 + nf, cg * 512:(cg + 1) * 512])])
                        for i in range(NT):
                            b = nb()

                            def fd(e, b=b, slab=slab, i=i, nf=nf):
                                ins = None
                                for fl in range(nf):
                                    ins = e.matmul(banks[b][:, :], lhsT=actT[:, fl, i * 128:(i + 1) * 128], rhs=slab[:, fl, :], start=(fl == 0), stop=(fl == nf - 1))
                                return ins
                            P.op("pe", fd, reads=wk + ["yT%d" % fl for fl in range(nf)], writes=["ps%d" % b])
                            P.op("dve", lambda e, b=b, i=i, cg=cg: e.tensor_tensor(out=h[:, i, cg * 512:(cg + 1) * 512], in0=h[:, i, cg * 512:(cg + 1) * 512],
                                 in1=banks[b][:, :], op=ALU.add), reads=["ps%d" % b, "h%d" % i], writes=["h%d" % i])
                    f0 += nf
                otoks = []
                for i in range(NT):
                    otoks.append(P.dma("sp", lambda e, i=i: e.dma_start(out=dst_d[i * 128:(i + 1) * 128, :], in_=h[:, i, :]), "spill%d" % i,
                                       reads=["h%d" % i], writes=["hs%d" % i if l < DRUN - 1 else "outd%d" % i]))
                P.fence()
                if l == DRUN - 1:
                    P.final_wait("sp", otoks)
        stopped = False
        for l in range(DRUN):
            if emit_layer(l) == 'stop':
                stopped = True
                break
        if stopped:
            P.fence()
            tok = P.dma("sp", lambda e: e.dma_start(out=out_d[0:128, 0:NPK], in_=pk[0][:, :]), "stopout", reads=["pk0"], writes=["outstop"])
            P.final_wait("sp", [tok])
        P.build()
    return nc


_CACHE = {}


def _consts():
    import ml_dtypes
    cbf = np.zeros((128, NCB), np.float32)
    s = np.arange(128)
    cbf[:, CB_ID:CB_ID + 128] = np.eye(128, dtype=np.float32)
    cbf[:, CB_ONE:CB_ONE + 128] = 1.0
    same = (s[:, None] // 64) == (s[None, :] // 64)
    cbf[:, CB_ML:CB_ML + 128] = (same & (s[:, None] <= s[None, :])).astype(np.float32)
    cbf[:, CB_MU:CB_MU + 128] = (same & (s[:, None] > s[None, :])).astype(np.float32)
    cbf[:, CB_MN:CB_MN + 128] = np.where(s[:, None] > s[None, :], NEG, 0.0)
    t = np.arange(1024)
    cbf[:, CB_CM:CB_CM + 1024] = (t % 64 != 0).astype(np.float32)[None, :]
    cbf[:, CB_EV:CB_EV + 1024] = ((t // 64) % 2 == 0).astype(np.float32)[None, :]
    cf = np.zeros((128, NCF), np.float32)
    cf[:, CF_TRI:CF_TRI + 128] = (s[:, None] <= s[None, :]).astype(np.float32)
    cf[:, CF_ONE:CF_ONE + 128] = 1.0
    pcs = []
    for r in range(2):
        pc = np.zeros((128, NPC), np.float32)
        pc[:, PC_ROLE] = float(r)
        pc[:, PC_NEGR] = NEG * (1 - r)
        for j in range(4):
            w = 2 << j
            tt = np.arange(16)
            cnt = np.minimum(tt + 1, w) if r == 0 else np.full(16, w)
            pc[:, PC_INVC + j * 16:PC_INVC + (j + 1) * 16] = (1.0 / cnt.astype(np.float32))[None, :]
        pcs.append(pc)
    return cbf, cf, pcs


def _pack_params(conv_w, pool_scale, gla_w_decay, gla_b_decay, gla_out_g, fox_q_g, fox_k_g, fox_b_f, norm_mix_g, norm_ffn_g):
    pk = np.zeros((DEPTH, 128, NPK), np.float32)
    for l in range(DEPTH):
        cw = conv_w[l][:, 0, :]
        pk[l, :, PK_CW:PK_CW + 12] = cw.reshape(3, 4, 128).transpose(2, 1, 0).reshape(128, 12)
        pk[l, :, PK_PSC:PK_PSC + 4] = pool_scale[l].reshape(4, 128).T
        pk[l, :, PK_BD:PK_BD + 2] = gla_b_decay[l].reshape(2, 128).T
        pk[l, :, PK_QG] = fox_q_g[l]
        pk[l, :, PK_KG] = fox_k_g[l]
        pk[l, :, PK_BF:PK_BF + 32] = np.tile(fox_b_f[l], 8)[None, :]
        pk[l, :, PK_GOG:PK_GOG + 128] = gla_out_g[l][None, :]
        pk[l, 0:16, PK_WD:PK_WD + 256] = gla_w_decay[l]
        pk[l, :, PK_GM:PK_GM + 16] = norm_mix_g[l].reshape(16, 128).T
        pk[l, :, PK_GF:PK_GF + 16] = norm_ffn_g[l].reshape(16, 128).T
    return pk


def kernel(x, norm_mix_g, w_in, conv_w, pool_w, pool_scale, gla_w_decay, gla_b_decay, gla_out_g,
           fox_q_g, fox_k_g, fox_b_f, w_out, norm_ffn_g, w_gate, w_up, w_down):
    f = lambda a: np.ascontiguousarray(np.asarray(a, dtype=np.float32))
    x = f(x)
    if "nc" not in _CACHE:
        _CACHE["nc"] = build_program()
    nc = _CACHE["nc"]
    cbf, cf, pcs = _consts()
    pk = _pack_params(f(conv_w), f(pool_scale), f(gla_w_decay), f(gla_b_decay), f(gla_out_g), f(fox_q_g), f(fox_k_g), f(fox_b_f),
                      f(norm_mix_g), f(norm_ffn_g))
    poolw = np.ascontiguousarray(f(pool_w).transpose(0, 2, 1, 3).reshape(DEPTH, 128, 512))
    w_in, w_out, w_gate, w_up, w_down = f(w_in), f(w_out), f(w_gate), f(w_up), f(w_down)
    in_maps = []
    for c in range(NRUN):
        b, r = c // 2, c % 2
        in_maps.append({
            "x": np.ascontiguousarray(x[b, r * T:(r + 1) * T, :]),
            "w_in": w_in, "w_out": w_out, "w_gate": w_gate, "w_up": w_up, "w_down": w_down,
            "pkp": pk, "poolwp": poolw, "cbf": cbf, "cf32": cf, "pcc": pcs[r],
        })
    t0 = _time.time()
    res = run_bass_kernel_spmd(nc, in_maps, core_ids=list(range(NRUN)))
    if os.environ.get("KVERBOSE"):
        print("run_bass_kernel_spmd seconds", _time.time() - t0)
    out = np.zeros((4, 2 * T, D), np.float32)
    for c in range(NRUN):
        b, r = c // 2, c % 2
        out[b, r * T:(r + 1) * T, :] = res.results[c]["out"]
    return out
```

```python
import contextlib
import numpy as np
import concourse.bass as bass
import concourse.mybir as mybir
from concourse.bass_utils import run_bass_kernel_spmd

F32 = mybir.dt.float32
BF16 = mybir.dt.bfloat16
AF = mybir.ActivationFunctionType
ALU = mybir.AluOpType

import os, time as _time
NCORES = 8
NRUN = int(os.environ.get('KCORES', '8'))
DRUN = int(os.environ.get('KDEPTH', '2'))
KDBG = bool(os.environ.get('KDBG'))
KSTOP = os.environ.get('KSTOP', '')


class StopEmit(Exception):
    pass
D = 2048
T = 1024
NT = 8
DEPTH = 2
INC = 5140
DFF = 5632
EPS = 1e-6
NEG = -30000.0
GLA_SCHED = [[], [('S', 0), ('A', 0)], [('B', 0), ('A', 1)], [('B', 1), ('A', 2)], [('B', 2), ('A', 3)], [('B', 3), ('A', 4)],
             [('B', 4), ('A', 5)], [('B', 5), ('A', 6), ('B', 6), ('A', 7), ('B', 7)]]

PK_CW = 0
PK_PSC = 12
PK_BD = 16
PK_QG = 18
PK_KG = 19
PK_BF = 20
PK_GOG = 52
PK_WD = 180
PK_GM = 436
PK_GF = 452
NPK = 468

CB_ID = 0
CB_ONE = 128
CB_ML = 256
CB_MU = 384
CB_MN = 512
CB_CM = 640
CB_EV = 1664
NCB = 2688
CF_TRI = 0
CF_ONE = 128
NCF = 256
PC_ROLE = 0
PC_NEGR = 1
PC_INVC = 2
NPC = 66

XF_NLF = 0
XF_A = 32
XF_B = 40
XF_S = 104
NXF = 360


DBG_OUTS = []


class Prog:
    ENGS = ("pe", "act", "dve", "pool", "sp")

    def __init__(self, nc, stack):
        self.nc = nc
        self.stack = stack
        self.h = {"pe": nc.tensor, "act": nc.scalar, "dve": nc.vector, "pool": nc.gpsimd, "sp": nc.sync}
        self.q = {e: [] for e in self.ENGS}
        self.semobj = {e: stack.enter_context(nc.semaphore("s_" + e)) for e in self.ENGS}
        self.cnt = {e: 0 for e in self.ENGS}
        self.known = {e: {} for e in self.ENGS}
        self.lastw = {}
        self.readers = {}
        self.nops = 0

    def dsem(self, name):
        if name not in self.semobj:
            self.semobj[name] = self.stack.enter_context(self.nc.semaphore("d_" + name))
            self.cnt[name] = 0
        return name

    def _deps(self, eng, reads, writes, sync_self):
        deps = {}

        def add(tok):
            if tok is None:
                return
            k, v = tok
            if deps.get(k, 0) < v:
                deps[k] = v

        for r in reads:
            add(self.lastw.get(r))
        for w in writes:
            add(self.lastw.get(w))
            for tok in self.readers.get(w, {}).items():
                add(tok)
        waits = []
        for k, v in deps.items():
            if k == eng and not sync_self:
                continue
            if self.known[eng].get(k, 0) >= v:
                continue
            self.known[eng][k] = v
            waits.append((k, v))
        return waits

    def _record(self, tok, reads, writes):
        k, v = tok
        for r in reads:
            d = self.readers.setdefault(r, {})
            if d.get(k, 0) < v:
                d[k] = v
        for w in writes:
            self.lastw[w] = tok
            self.readers[w] = {}

    def op(self, eng, fn, reads=(), writes=(), sync_self=None):
        if sync_self is None:
            sync_self = eng != "pe"
        writes = list(writes) + [r for r in reads if r.startswith("ps")]
        reads = [r for r in reads if not r.startswith("ps")]
        waits = self._deps(eng, reads, writes, sync_self)
        self.cnt[eng] += 1
        tok = (eng, self.cnt[eng])
        self.q[eng].append((waits, fn, eng, 1))
        self._record(tok, reads, writes)
        self.nops += 1
        return tok

    def dma(self, queue, fn, dsem, reads=(), writes=(), inc=16):
        self.dsem(dsem)
        waits = self._deps(queue, reads, writes, True)
        self.cnt[dsem] += inc
        tok = (dsem, self.cnt[dsem])
        self.q[queue].append((waits, fn, dsem, inc))
        self._record(tok, reads, writes)
        return tok

    def fence(self):
        snap = {k: v for k, v in self.cnt.items() if v > 0 and not k.startswith("cc_")}
        for e in self.ENGS:
            if e == "pool":
                continue
            waits = []
            for k, v in snap.items():
                if k == e:
                    continue
                if self.known[e].get(k, 0) >= v:
                    continue
                self.known[e][k] = v
                waits.append((k, v))
            if waits:
                self.q[e].append((waits, None, None, 0))

    def final_wait(self, eng, toks):
        waits = [t for t in toks if t is not None]
        self.q[eng].append((waits, None, None, 0))

    def build(self):
        with self.nc.Block() as block:
            secs = {"pe": block.tensor, "act": block.scalar, "dve": block.vector, "pool": block.gpsimd, "sp": block.sync}
            for e in self.ENGS:
                items = self.q[e]
                semobj = self.semobj

                def body(engh, items=items):
                    for waits, fn, semk, inc in items:
                        for k, v in waits:
                            engh.wait_ge(semobj[k], v)
                        if fn is not None:
                            ins = fn(engh)
                            ins.then_inc(semobj[semk], inc)

                secs[e](body)


def build_program():
    nc = bass.Bass("TRN2", target_bir_lowering=False)
    dt_in = lambda name, shape: nc.dram_tensor(name, shape, F32, kind="ExternalInput").ap()
    x_d = dt_in("x", [T, D])
    w_in_d = dt_in("w_in", [DEPTH, D, INC])
    if os.environ.get('KSMALLW'):
        w_out_d = w_gate_d = w_up_d = w_down_d = None
    else:
        w_out_d = dt_in("w_out", [DEPTH, D, D])
        w_gate_d = dt_in("w_gate", [DEPTH, D, DFF])
        w_up_d = dt_in("w_up", [DEPTH, D, DFF])
        w_down_d = dt_in("w_down", [DEPTH, DFF, D])
    pk_d = dt_in("pkp", [DEPTH, 128, NPK])
    poolw_d = dt_in("poolwp", [DEPTH, 128, 512])
    cb_d = dt_in("cbf", [128, NCB])
    cf_d = dt_in("cf32", [128, NCF])
    pc_d = dt_in("pcc", [128, NPC])
    out_d = nc.dram_tensor("out", [T, D], F32, kind="ExternalOutput").ap()
    hs_d = nc.dram_tensor("hs", [T, D], F32).ap()
    if KDBG:
        dbg_uT = nc.dram_tensor("dbg_uT", [128, 16 * T], BF16, kind="ExternalOutput").ap()
        dbg_yT = nc.dram_tensor("dbg_yT", [128, 16 * T], BF16, kind="ExternalOutput").ap()
        dbg_h1 = nc.dram_tensor("dbg_h1", [128, NT * D], F32, kind="ExternalOutput").ap()
    xs_kv = nc.dram_tensor("xs_kv", [128, 8192], BF16).ap()
    xr_kv = nc.dram_tensor("xr_kv", [256, 8192], BF16).ap()
    xs_f1 = nc.dram_tensor("xs_f1", [128, XF_S], F32).ap()
    xr_f1 = nc.dram_tensor("xr_f1", [256, XF_S], F32).ap()
    xs_f2 = nc.dram_tensor("xs_f2", [128, 256], F32).ap()
    xr_f2 = nc.dram_tensor("xr_f2", [256, 256], F32).ap()

    with contextlib.ExitStack() as stack:
        P = Prog(nc, stack)
        dbg_n = [0]

        def dbg(name, ap, shape, dt, reads):
            if not KDBG:
                return
            d = nc.dram_tensor("dbg_" + name, list(shape), dt, kind="ExternalOutput").ap()
            DBG_OUTS.append(("dbg_" + name, list(shape), dt))
            dbg_n[0] += 1
            P.dma("sp", lambda e, ap=ap, d=d: e.dma_start(out=d, in_=ap), "dbgx%d" % dbg_n[0], reads=reads, writes=["dbgx%d" % dbg_n[0]])
        sb = lambda name, shape, dt: stack.enter_context(nc.sbuf_tensor(name, shape, dt))
        NSLAB = 3
        wring = [sb("wring%d" % i, [128, 16, 512], BF16) for i in range(NSLAB)]
        yT = sb("yT", [128, 16, T], BF16)
        cb = sb("cb", [128, NCB], BF16)
        cf = sb("cf", [128, NCF], F32)
        pc = sb("pc", [128, NPC], F32)
        pk = [sb("pk%d" % l, [128, NPK], F32) for l in range(DEPTH)]
        poolw = [sb("poolw%d" % l, [128, 4, 128], BF16) for l in range(DEPTH)]
        xn = sb("xn", [128, D], BF16)
        sm = sb("sm", [128, 64], F32)
        banks = [stack.enter_context(nc.psum_tensor("bank%d" % i, [128, 512], F32)) for i in range(8)]
        bank_rr = [0]

        def nb():
            b = bank_rr[0]
            bank_rr[0] = (b + 1) % 8
            return b

        ident = cb[:, CB_ID:CB_ID + 128]
        ones_b = cb[:, CB_ONE:CB_ONE + 128]
        ml_b = cb[:, CB_ML:CB_ML + 128]
        mu_b = cb[:, CB_MU:CB_MU + 128]
        mneg_b = cb[:, CB_MN:CB_MN + 128]
        cmask = cb[:, CB_CM:CB_CM + 1024]
        evmask = cb[:, CB_EV:CB_EV + 1024]
        tri_f = cf[:, CF_TRI:CF_TRI + 128]
        ones_f = cf[:, CF_ONE:CF_ONE + 128]
        role = pc[:, PC_ROLE:PC_ROLE + 1]
        negrole = pc[:, PC_NEGR:PC_NEGR + 1]

        P.dma("pool", lambda e: e.dma_start(out=cb[:, :], in_=cb_d[:, :]), "c_cb", writes=["cb"])
        P.dma("sp", lambda e: e.dma_start(out=cf[:, :], in_=cf_d[:, :]), "c_cf", writes=["cf"])
        P.dma("sp", lambda e: e.dma_start(out=pc[:, :], in_=pc_d[:, :]), "c_pc", writes=["pc"])
        for l in range(DEPTH):
            P.dma("sp", lambda e, l=l: e.dma_start(out=pk[l][:, :], in_=pk_d[l, :, :]), "c_pk%d" % l, writes=["pk%d" % l])
            P.dma("pool", lambda e, l=l: e.dma_start(out=poolw[l][:, :, :].rearrange("p a b -> p (a b)"), in_=poolw_d[l, :, :]),
                  "c_pw%d" % l, writes=["poolw%d" % l])

        wstate = {"n": 0}

        def load_slab(pieces):
            slot = wstate["n"] % NSLAB
            wstate["n"] += 1
            slab = wring[slot]
            key = "w%d" % slot
            tok = None
            first = wstate["n"] == 1
            for i, (vf, src) in enumerate(pieces):
                tok = P.dma("pool", lambda e, vf=vf, src=src, slab=slab: e.dma_start(out=vf(slab), in_=src),
                            "wsem%d" % slot, writes=[key] if i == 0 else [], reads=["hin1"] if (first and i == 0) else [])
            P.lastw[key] = tok
            return slab, [key]

        def fm_group(uT, bank, slab, wkeys, m0, msz, hf, ukey="uT"):
            def fn(e):
                ins = None
                for kc in range(16):
                    ins = e.matmul(banks[bank][0:msz, 0:512], lhsT=slab[:, kc, m0:m0 + msz],
                                   rhs=uT[:, kc, hf * 512:(hf + 1) * 512], start=(kc == 0), stop=(kc == 15))
                return ins
            return P.op("pe", fn, reads=list(wkeys) + [ukey], writes=["ps%d" % bank])

        def tm_group(uT, bank, slab, wkeys, n0, nsz, i, c0=0, ukey="uT"):
            def fn(e):
                ins = None
                for kc in range(16):
                    ins = e.matmul(banks[bank][:, c0:c0 + nsz], lhsT=uT[:, kc, i * 128:(i + 1) * 128],
                                   rhs=slab[:, kc, n0:n0 + nsz], start=(kc == 0), stop=(kc == 15))
                return ins
            return P.op("pe", fn, reads=list(wkeys) + [ukey], writes=["ps%d" % bank])

        def rmsnorm_to_uT(uT, l, gcol, get_tile, xn2, xn2keys):
            bufs = [(xn[:, :], ["xn"]), (xn2, list(xn2keys))]
            for i in range(NT):
                src, skey = get_tile(i)
                xb, xk = bufs[i % 2]
                P.op("act", lambda e, src=src, i=i, xb=xb: e.activation(out=xb, in_=src, func=AF.Square, accum_out=sm[:, i:i + 1]),
                     reads=[skey], writes=xk + ["sm_ss%d" % i])
                P.op("act", lambda e, i=i: e.activation(out=sm[:, 8 + i:9 + i], in_=sm[:, i:i + 1], func=AF.Sqrt, scale=1.0 / D, bias=eps_col),
                     reads=["sm_ss%d" % i, "epsc"], writes=["sm_rt%d" % i])
                P.op("dve", lambda e, i=i: e.reciprocal(out=sm[:, 16 + i:17 + i], in_=sm[:, 8 + i:9 + i]),
                     reads=["sm_rt%d" % i], writes=["sm_rs%d" % i])
                if i % 4 != 3:
                    P.op("act", lambda e, src=src, i=i, xb=xb: e.activation(out=xb, in_=src, func=AF.Copy, scale=sm[:, 16 + i:17 + i]),
                         reads=[skey, "sm_rs%d" % i], writes=xk)
                else:
                    P.op("dve", lambda e, src=src, i=i, xb=xb: e.tensor_scalar(out=xb, in0=src, scalar1=sm[:, 16 + i:17 + i], scalar2=None, op0=ALU.mult),
                         reads=[skey, "sm_rs%d" % i], writes=xk)
                for half in range(2):
                    b = nb()

                    def fn(e, b=b, half=half, xb=xb):
                        ins = None
                        pv = banks[b][:, :].bitcast(BF16)
                        for cc in range(8):
                            c = half * 8 + cc
                            ins = e.transpose(out=pv[:, cc * 128:(cc + 1) * 128], in_=xb[:, c * 128:(c + 1) * 128], identity=ident)
                        return ins
                    P.op("pe", fn, reads=xk + ["cb"], writes=["ps%d" % b])
                    P.op("dve", lambda e, b=b, half=half, i=i: e.tensor_tensor(
                        out=uT[:, half * 8:(half + 1) * 8, i * 128:(i + 1) * 128],
                        in0=banks[b][:, :].bitcast(BF16).rearrange("p (a t) -> p a t", a=8),
                        in1=pk[l][:, gcol + half * 8:gcol + (half + 1) * 8].unsqueeze(2).to_broadcast([128, 8, 128]),
                        op=ALU.mult), reads=["ps%d" % b, "pk%d" % l], writes=["uT"])

        eps_col = sm[:, 60:61]
        P.op("dve", lambda e: e.memset(eps_col, EPS), writes=["epsc"])
        one_col = sm[:, 61:62]
        P.op("dve", lambda e: e.memset(one_col, 1.0), writes=["onec"])

        def emit_layer(l):
            src_d = x_d if l == 0 else hs_d
            dst_d = hs_d if l < DRUN - 1 else out_d
            srckey = "x" if l == 0 else "hs"
            pkl = pk[l]
            pkk = "pk%d" % l
            win = w_in_d[l].rearrange("(c p) n -> p c n", p=128)
            with contextlib.ExitStack() as mstack:
                msb = lambda name, shape, dt: mstack.enter_context(nc.sbuf_tensor("%s_l%d" % (name, l), shape, dt))
                gqT = msb("gqT", [128, 2, T], BF16)
                gkT = msb("gkT", [128, 2, T], BF16)
                aT = msb("aT", [16, T], F32)
                gv_tm = msb("gv_tm", [128, NT, 512], BF16)
                sgg = msb("sgg", [128, NT, 512], BF16)
                qT = msb("qT", [128, 4, T], BF16)
                nlf = msb("nlf", [128, 32], F32)
                fixA_cb = msb("fixA_cb", [128, 4, 2], F32)
                ufix = msb("ufix", [128, 4, 4], F32)
                pfix = msb("pfix", [128, 4, 32], F32)
                snd = msb("snd", [128, NXF], F32)
                su = mstack.enter_context(contextlib.ExitStack())
                uT = su.enter_context(nc.sbuf_tensor("uTm_l%d" % l, [128, 16, T], BF16))
                with contextlib.ExitStack() as s1:
                    hin = [s1.enter_context(nc.sbuf_tensor("hin%d_l%d" % (i, l), [128, D], F32)) for i in range(4)]

                    def get_tile(i, hin=hin):
                        buf = hin[i % 4]
                        P.dma("sp", lambda e, buf=buf, i=i: e.dma_start(out=buf[:, :], in_=src_d[i * 128:(i + 1) * 128, :]),
                              "hin%d" % (i % 4), reads=["%s%d" % (srckey, i)], writes=["hin%d" % (i % 4)])
                        return buf[:, :], "hin%d" % (i % 4)
                    xnB = s1.enter_context(nc.sbuf_tensor("xnB_l%d" % l, [128, D], BF16))
                    rmsnorm_to_uT(uT, l, PK_GM, get_tile, xnB[:, :], ["xnB"])
                if KDBG and l == 0:
                    P.dma("sp", lambda e, uT=uT: e.dma_start(out=dbg_uT[:, :], in_=uT[:, :, :].rearrange("p a t -> p (a t)")), "dbg1", reads=["uT"], writes=["dbg_uT"])
                if KSTOP == 'n1':
                    return 'stop'
                P.fence()
                with contextlib.ExitStack() as s2:
                    ssb = lambda name, shape, dt: s2.enter_context(nc.sbuf_tensor("%s_l%d" % (name, l), shape, dt))
                    t_cc = ssb("t_cc", [128, 512], F32)
                    uA = ssb("uA", [128, T + 2], F32)
                    acc = ssb("acc", [128, 512], F32)
                    pB = ssb("pB", [128, T + 16], F32)
                    sA = ssb("sA", [128, T + 16], F32)
                    sB = ssb("sB", [128, T + 16], F32)
                    dT = ssb("dT", [128, T], BF16)
                    sq = ssb("sq", [128, 512], BF16)
                    qraw = ssb("qraw", [128, 512], F32)
                    rt2 = [ssb("rt%d" % i, [128, 512], F32) for i in range(2)]
                    qraw2 = [qraw, ssb("qraw1", [128, 512], F32)]
                    kst = [ssb("kst%d" % i, [128, T], BF16) for i in range(2)]
                    vst = [ssb("vst%d" % i, [128, 512], BF16) for i in range(2)]
                    xf = ssb("xf", [128, 32], F32)

                    P.op("dve", lambda e: e.memset(uA[:, 0:2], 0.0), writes=["uA"])
                    P.op("dve", lambda e: e.memset(pB[:, 0:16], 0.0), writes=["pB"])
                    P.op("dve", lambda e: e.memset(sA[:, 0:16], 0.0), writes=["sA"])
                    P.op("dve", lambda e: e.memset(sB[:, 0:16], 0.0), writes=["sB"])
                    for j in range(4):
                        slab, wk = load_slab([
                            (lambda s, g=g: s[:, :, g * 128:(g + 1) * 128], win[:, :, g * 512 + j * 128: g * 512 + (j + 1) * 128])
                            for g in range(3)])
                        for hf in range(2):
                            b0, b1, b2 = nb(), nb(), nb()
                            fm_group(uT, b0, slab, wk, 0, 128, hf)
                            fm_group(uT, b1, slab, wk, 128, 128, hf)
                            fm_group(uT, b2, slab, wk, 256, 128, hf)
                            c0 = hf * 512
                            P.op("act", lambda e, b1=b1: e.activation(out=t_cc[:, :], in_=banks[b1][:, :], func=AF.Copy),
                                 reads=["ps%d" % b1], writes=["t_cc"])
                            P.op("dve", lambda e, b2=b2, c0=c0: e.tensor_tensor(out=uA[:, 2 + c0:2 + c0 + 512], in0=t_cc[:, :], in1=banks[b2][:, :], op=ALU.mult),
                                 reads=["t_cc", "ps%d" % b2], writes=["uA"])
                            P.op("dve", lambda e, c0=c0, j=j: e.tensor_scalar(out=acc[:, :], in0=uA[:, 2 + c0:2 + c0 + 512],
                                 scalar1=pkl[:, PK_CW + j * 3 + 2:PK_CW + j * 3 + 3], scalar2=None, op0=ALU.mult),
                                 reads=["uA", pkk], writes=["acc"])
                            P.op("dve", lambda e, c0=c0, j=j: e.scalar_tensor_tensor(out=acc[:, :], in0=uA[:, 1 + c0:1 + c0 + 512],
                                 scalar=pkl[:, PK_CW + j * 3 + 1:PK_CW + j * 3 + 2], in1=acc[:, :], op0=ALU.mult, op1=ALU.add),
                                 reads=["uA", pkk, "acc"], writes=["acc"])
                            P.op("dve", lambda e, c0=c0, j=j: e.scalar_tensor_tensor(out=acc[:, :], in0=uA[:, c0:c0 + 512],
                                 scalar=pkl[:, PK_CW + j * 3:PK_CW + j * 3 + 1], in1=acc[:, :], op0=ALU.mult, op1=ALU.add),
                                 reads=["uA", pkk, "acc"], writes=["acc"])
                            P.op("dve", lambda e, b0=b0, c0=c0, j=j: e.tensor_tensor(out=yT[:, j, c0:c0 + 512], in0=acc[:, :], in1=banks[b0][:, :], op=ALU.mult),
                                 reads=["acc", "ps%d" % b0], writes=["yT%d" % j])
                            if hf == 0:
                                P.op("act", lambda e, b0=b0, j=j: e.activation(out=fixA_cb[:, j, :], in_=banks[b0][:, 0:2], func=AF.Copy),
                                     reads=["ps%d" % b0], writes=["fixA_cb"])
                                P.op("act", lambda e, j=j: e.activation(out=ufix[:, j, 2:4], in_=uA[:, 2:4], func=AF.Copy),
                                     reads=["uA"], writes=["ufix"])
                            else:
                                P.op("act", lambda e, j=j: e.activation(out=snd[:, XF_A + j * 2:XF_A + j * 2 + 2], in_=uA[:, T:T + 2], func=AF.Copy),
                                     reads=["uA"], writes=["snd_a%d" % j])
                    if KSTOP == 'ipA':
                        return 'stop'
                    slab, wk = load_slab([(lambda s: s[:, :, :], win[:, :, 1536:2048])])
                    qslab, qwk = load_slab([(lambda s: s[:, :, :], win[:, :, 2048:2560])])

                    def gqk_groups(m):
                        dst = gqT if m < 2 else gkT
                        dkey = "gqT" if m < 2 else "gkT"
                        for hf in range(2):
                            b = nb()
                            fm_group(uT, b, qslab, qwk, m * 128, 128, hf)
                            P.op("act", lambda e, b=b, dst=dst, m=m, hf=hf: e.activation(out=dst[:, m % 2, hf * 512:(hf + 1) * 512], in_=banks[b][:, :], func=AF.Copy),
                                 reads=["ps%d" % b], writes=[dkey])
                    for j in range(4):
                        for hf in range(2):
                            b = nb()
                            fm_group(uT, b, slab, wk, j * 128, 128, hf)
                            P.op("act", lambda e, b=b, hf=hf: e.activation(out=pB[:, 16 + hf * 512:16 + (hf + 1) * 512], in_=banks[b][:, :], func=AF.Copy),
                                 reads=["ps%d" % b], writes=["pB"])
                        cur, curk = pB, "pB"
                        pp = [(sA, "sA"), (sB, "sB")]
                        for st in range(j + 1):
                            sh = 1 << st
                            dst, dstk = pp[st % 2]
                            P.op("dve", lambda e, cur=cur, dst=dst, sh=sh: e.tensor_tensor(out=dst[:, 16:16 + T], in0=cur[:, 16:16 + T],
                                 in1=cur[:, 16 - sh:16 - sh + T], op=ALU.add), reads=[curk], writes=[dstk])
                            cur, curk = dst, dstk
                        w = 2 << j
                        P.op("dve", lambda e, cur=cur, w=w: e.scalar_tensor_tensor(out=dT[:, :], in0=cur[:, 16:16 + T], scalar=1.0 / w,
                             in1=pB[:, 16:16 + T], op0=ALU.mult, op1=ALU.subtract), reads=[curk, "pB"], writes=["dT"])
                        P.op("act", lambda e, j=j: e.activation(out=pfix[:, j, 16:32], in_=pB[:, 16:32], func=AF.Copy), reads=["pB"], writes=["pfix"])
                        P.op("act", lambda e, j=j: e.activation(out=snd[:, XF_B + j * 16:XF_B + (j + 1) * 16], in_=pB[:, T:T + 16], func=AF.Copy),
                             reads=["pB"], writes=["snd_b%d" % j])
                        gqk_groups(j)
                        for hf in range(2):
                            b = nb()
                            P.op("pe", lambda e, b=b, j=j, hf=hf: e.matmul(banks[b][:, :], lhsT=poolw[l][:, j, :], rhs=dT[:, hf * 512:(hf + 1) * 512],
                                 start=True, stop=True), reads=["dT", "poolw%d" % l], writes=["ps%d" % b])
                            P.op("act", lambda e, b=b, j=j, hf=hf: e.activation(out=yT[:, 4 + j, hf * 512:(hf + 1) * 512], in_=banks[b][:, :],
                                 func=AF.Copy, scale=pkl[:, PK_PSC + j:PK_PSC + j + 1]), reads=["ps%d" % b, pkk], writes=["yT%d" % (4 + j)])
                    if KSTOP == 'ipB':
                        return 'stop'
                    if KSTOP == 'ipQK':
                        return 'stop'
                    slab, wk = load_slab([(lambda s: s[:, :, :], win[:, :, 2560:3072])])
                    for i in range(NT):
                        b = nb()
                        tm_group(uT, b, slab, wk, 0, 512, i)
                        P.op("act", lambda e, b=b, i=i: e.activation(out=gv_tm[:, i, :], in_=banks[b][:, :], func=AF.Copy),
                             reads=["ps%d" % b], writes=["gv_tm"])
                    slab, wk = load_slab([(lambda s: s[:, :, :], win[:, :, 3072:3584])])
                    for i in range(NT):
                        b = nb()
                        tm_group(uT, b, slab, wk, 0, 512, i)
                        P.op("act", lambda e, b=b: e.activation(out=qraw[:, :], in_=banks[b][:, :], func=AF.Silu),
                             reads=["ps%d" % b], writes=["qraw0"])
                        P.op("dve", lambda e, i=i: e.tensor_tensor(out=sgg[:, i, :].rearrange("p (h v) -> p h v", h=4),
                             in0=qraw[:, :].rearrange("p (h v) -> p h v", h=4),
                             in1=pkl[:, PK_GOG:PK_GOG + 128].unsqueeze(1).to_broadcast([128, 4, 128]), op=ALU.mult),
                             reads=["qraw0", pkk], writes=["sgg"])
                    if KSTOP == 'ipV':
                        return 'stop'
                    slab, wk = load_slab([(lambda s: s[:, :, 0:128], win[:, :, 3472:3600]), (lambda s: s[:, :, 128:256], win[:, :, 5012:5140])])
                    for hf in range(2):
                        b = nb()
                        fm_group(uT, b, slab, wk, 112, 16, hf)
                        P.op("act", lambda e, b=b, hf=hf: e.activation(out=aT[:, hf * 512:(hf + 1) * 512], in_=banks[b][0:16, :], func=AF.Copy),
                             reads=["ps%d" % b], writes=["aT"])
                    b = nb()
                    for i in range(NT):
                        tm_group(uT, b, slab, wk, 240, 16, i, c0=i * 16)
                    P.op("dve", lambda e, b=b: e.tensor_copy(out=xf[:, :].rearrange("p (i h) -> p i h", h=4),
                         in_=banks[b][:, 0:128].rearrange("p (i c) -> p i c", c=16)[:, :, 12:16]), reads=["ps%d" % b], writes=["xf"])
                    P.op("dve", lambda e: e.tensor_tensor(out=xf[:, :], in0=xf[:, :], in1=pkl[:, PK_BF:PK_BF + 32], op=ALU.add),
                         reads=["xf", pkk], writes=["xf"])
                    P.op("act", lambda e: e.activation(out=xf[:, :], in_=xf[:, :], func=AF.Exp, scale=-1.0), reads=["xf"], writes=["xf"])
                    P.op("act", lambda e: e.activation(out=nlf[:, :], in_=xf[:, :], func=AF.Ln, bias=one_col), reads=["xf", "onec"], writes=["nlf"])
                    P.op("dve", lambda e: e.tensor_copy(out=snd[:, XF_NLF:XF_NLF + 32], in_=nlf[:, :]), reads=["nlf"], writes=["snd_nlf"])
                    sndkeys1 = ["snd_a%d" % j for j in range(4)] + ["snd_b%d" % j for j in range(4)] + ["snd_nlf"]
                    P.dma("sp", lambda e: e.dma_start(out=xs_f1[:, :], in_=snd[:, 0:XF_S]), "snd1", reads=sndkeys1, writes=["xs_f1"])
                    P.dma("pool", lambda e: e.collective_compute("AllGather", ALU.bypass, replica_groups=[[2 * i, 2 * i + 1] for i in range(NRUN // 2)],
                          ins=[xs_f1[:, :]], outs=[xr_f1[:, :]]), "cc_f1", reads=["xs_f1"], writes=["xr_f1"], inc=1)
                    for which in range(2):
                        c0w = 3600 + which * 512
                        slab, wk = load_slab([(lambda s: s[:, :, :], win[:, :, c0w:c0w + 512])])
                        gcolq = PK_QG + which
                        items = [(m, hf) for m in range(4) for hf in range(2)]
                        fb = {0: nb()}
                        fm_group(uT, fb[0], slab, wk, 0, 128, 0)
                        for k, (m, hf) in enumerate(items):
                            b = fb[k]
                            qr, qrk = qraw2[k % 2], "qraw%d" % (k % 2)
                            rtt, rtk = rt2[k % 2], "rt%d" % (k % 2)
                            P.op("act", lambda e, b=b: e.activation(out=sq[:, :], in_=banks[b][:, :], func=AF.Square),
                                 reads=["ps%d" % b], writes=["sq"])
                            P.op("act", lambda e, b=b, qr=qr: e.activation(out=qr[:, :], in_=banks[b][:, :], func=AF.Copy),
                                 reads=["ps%d" % b], writes=[qrk])
                            if k + 1 < len(items):
                                fb[k + 1] = nb()
                                fm_group(uT, fb[k + 1], slab, wk, items[k + 1][0] * 128, 128, items[k + 1][1])
                            b2 = nb()
                            P.op("pe", lambda e, b2=b2: e.matmul(banks[b2][:, :], lhsT=ones_b, rhs=sq[:, :], start=True, stop=True),
                                 reads=["sq", "cb"], writes=["ps%d" % b2])
                            P.op("act", lambda e, b2=b2, rtt=rtt: e.activation(out=rtt[:, :], in_=banks[b2][:, :], func=AF.Ln, scale=1.0 / 128, bias=eps_col),
                                 reads=["ps%d" % b2, "epsc"], writes=[rtk])
                            P.op("act", lambda e, rtt=rtt: e.activation(out=rtt[:, :], in_=rtt[:, :], func=AF.Exp, scale=-0.5), reads=[rtk], writes=[rtk])
                            if which == 0:
                                P.op("dve", lambda e, m=m, hf=hf, gcolq=gcolq, qr=qr, rtt=rtt: e.scalar_tensor_tensor(out=qT[:, m, hf * 512:(hf + 1) * 512], in0=qr[:, :],
                                     scalar=pkl[:, gcolq:gcolq + 1], in1=rtt[:, :], op0=ALU.mult, op1=ALU.mult),
                                     reads=[qrk, rtk, pkk], writes=["qT"])
                            else:
                                ks = kst[m % 2]
                                P.op("dve", lambda e, ks=ks, hf=hf, gcolq=gcolq, qr=qr, rtt=rtt: e.scalar_tensor_tensor(out=ks[:, hf * 512:(hf + 1) * 512], in0=qr[:, :],
                                     scalar=pkl[:, gcolq:gcolq + 1], in1=rtt[:, :], op0=ALU.mult, op1=ALU.mult),
                                     reads=[qrk, rtk, pkk], writes=["kst%d" % (m % 2)])
                                if hf == 1:
                                    P.dma("sp", lambda e, ks=ks, m=m: e.dma_start(out=xs_kv[:, m * 1024:(m + 1) * 1024], in_=ks[:, :]),
                                          "kst%d" % (m % 2), reads=["kst%d" % (m % 2)], writes=["xs_k%d" % m])
                    if KSTOP == 'ipF':
                        return 'stop'
                    slab, wk = load_slab([(lambda s: s[:, :, :], win[:, :, 4624:5136])])
                    for i in range(NT):
                        b = nb()
                        tm_group(uT, b, slab, wk, 0, 512, i)
                        vs = vst[i % 2]
                        P.op("act", lambda e, b=b, vs=vs: e.activation(out=vs[:, :], in_=banks[b][:, :], func=AF.Copy),
                             reads=["ps%d" % b], writes=["vst%d" % (i % 2)])
                        P.dma("sp", lambda e, vs=vs, i=i: e.dma_start(
                            out=xs_kv[:, 4096:8192].rearrange("p (h i d) -> p h i d", h=4, i=8)[:, :, i, :],
                            in_=vs[:, :].rearrange("p (h d) -> p h d", h=4)),
                            "vst%d" % (i % 2), reads=["vst%d" % (i % 2)], writes=["xs_v%d" % i])
                    if KSTOP == 'ipFV':
                        return 'stop'
                    P.dma("pool", lambda e: e.collective_compute("AllGather", ALU.bypass, replica_groups=[[2 * i, 2 * i + 1] for i in range(NRUN // 2)],
                          ins=[xs_kv[:, :]], outs=[xr_kv[:, :]]), "cc_kv", reads=["xs_k%d" % m for m in range(4)] + ["xs_v%d" % i for i in range(NT)],
                          writes=["xr_kv"], inc=1)
                if l == 0:
                    dbg("gqT", gqT[:, :, :].rearrange("p a t -> p (a t)"), [128, 2 * T], BF16, ["gqT"])
                    dbg("gkT", gkT[:, :, :].rearrange("p a t -> p (a t)"), [128, 2 * T], BF16, ["gkT"])
                    dbg("aT", aT[:, :], [16, T], F32, ["aT"])
                    dbg("gv", gv_tm[:, :, :].rearrange("p a t -> p (a t)"), [128, NT * 512], BF16, ["gv_tm"])
                    dbg("sgg", sgg[:, :, :].rearrange("p a t -> p (a t)"), [128, NT * 512], BF16, ["sgg"])
                P.fence()
                su.close()
                if KSTOP == 'ip':
                    return 'stop'
                with contextlib.ExitStack() as s3:
                    gsb = lambda name, shape, dt: s3.enter_context(nc.sbuf_tensor("%s_l%d" % (name, l), shape, dt))
                    qd0 = gsb("qd0", [128, 2, T], BF16)
                    qd1 = gsb("qd1", [128, 2, T], BF16)
                    qG = gsb("qG", [128, 2, T], BF16)
                    ST = gsb("ST", [128, NT, 4, 128], BF16)
                    S_bf = gsb("S_bf", [128, 16, 2, 128], BF16)
                    S_f2 = gsb("S_f", [128, 2, 2, 128], F32)
                    elast = gsb("elast", [128, 2, 16], F32)
                    negbd = gsb("negbd", [128, 2], F32)
                    P.op("dve", lambda e: e.tensor_scalar(out=negbd[:, :], in0=pkl[:, PK_BD:PK_BD + 2], scalar1=-1.0, scalar2=None, op0=ALU.mult),
                         reads=[pkk], writes=["negbd"])
                    P.op("dve", lambda e: e.memset(S_f2[:, :, :, :], 0.0), writes=["S_f0", "S_f1"])
                    with contextlib.ExitStack() as s4:
                        tsb = lambda name, shape, dt: s4.enter_context(nc.sbuf_tensor("%s_l%d" % (name, l), shape, dt))
                        sp_ = tsb("sp_", [128, T], F32)
                        ccn = tsb("ccn", [128, T], F32)
                        Gn = tsb("Gn", [128, T], F32)
                        ex = tsb("ex", [128, T], F32)
                        qd = tsb("qd", [128, T], BF16)
                        kd = tsb("kd", [128, T], BF16)
                        qu = tsb("qu", [128, T], BF16)
                        ku = tsb("ku", [128, T], BF16)
                        klT = tsb("klT", [128, T], BF16)
                        kl_tm = tsb("kl_tm", [128, NT, 128], BF16)
                        t1 = tsb("t1", [128, 512], F32)
                        t2 = tsb("t2", [128, 512], F32)
                        wdec = pkl[0:16, PK_WD:PK_WD + 256]
                        for c in range(2):
                            for hf in range(2):
                                b = nb()
                                P.op("pe", lambda e, b=b, c=c, hf=hf: e.matmul(banks[b][:, :], lhsT=wdec[:, c * 128:(c + 1) * 128],
                                     rhs=aT[:, hf * 512:(hf + 1) * 512], start=True, stop=True), reads=[pkk, "aT"], writes=["ps%d" % b])
                                P.op("act", lambda e, b=b, c=c, hf=hf: e.activation(out=ex[:, hf * 512:(hf + 1) * 512], in_=banks[b][:, :], func=AF.Exp,
                                     scale=-1.0, bias=negbd[:, c:c + 1]), reads=["ps%d" % b, "negbd"], writes=["ex"])
                            P.op("act", lambda e: e.activation(out=sp_[:, :], in_=ex[:, :], func=AF.Ln, bias=one_col), reads=["ex", "onec"], writes=["sp_"])
                            P.op("dve", lambda e: e.tensor_tensor_scan(out=ccn[:, :], data0=cmask, data1=sp_[:, :], initial=0.0, op0=ALU.mult, op1=ALU.add),
                                 reads=["sp_", "cb"], writes=["ccn"])
                            P.op("dve", lambda e: e.tensor_tensor_scan(out=Gn[:, :], data0=one_col.to_broadcast([128, T]), data1=sp_[:, :], initial=0.0,
                                 op0=ALU.mult, op1=ALU.add), reads=["sp_", "onec"], writes=["Gn"])
                            if KSTOP == 'g1':
                                return 'stop'
                            lastn = ccn[:, :].rearrange("p (n s) -> p n s", s=64)[:, :, 63:64]
                            P.op("act", lambda e: e.activation(out=ex[:, :], in_=ccn[:, :], func=AF.Exp, scale=-1.0 / 16), reads=["ccn"], writes=["ex"])
                            P.op("act", lambda e: e.activation(out=sp_[:, :], in_=ccn[:, :], func=AF.Exp, scale=1.0 / 16), reads=["ccn"], writes=["sp_"])
                            P.op("act", lambda e: e.activation(out=Gn[:, :], in_=Gn[:, :], func=AF.Exp, scale=-1.0 / 16), reads=["Gn"], writes=["Gn"])
                            P.op("act", lambda e, c=c, lastn=lastn: e.activation(out=elast[:, c, :], in_=lastn.rearrange("p n s -> p (n s)"), func=AF.Exp, scale=-1.0 / 16),
                                 reads=["ccn"], writes=["elast"])
                            P.op("dve", lambda e, c=c: e.scalar_tensor_tensor(out=qd[:, :], in0=gqT[:, c, :], scalar=0.125, in1=ex[:, :], op0=ALU.mult, op1=ALU.mult),
                                 reads=["gqT", "ex"], writes=["qd"])
                            P.op("dve", lambda e, c=c: e.tensor_tensor(out=ku[:, :], in0=gkT[:, c, :], in1=ex[:, :], op=ALU.mult),
                                 reads=["gkT", "ex"], writes=["ku"])
                            P.op("dve", lambda e, lastn=lastn: e.tensor_tensor(out=ex[:, :].rearrange("p (n s) -> p n s", s=64),
                                 in0=ccn[:, :].rearrange("p (n s) -> p n s", s=64), in1=lastn.to_broadcast([128, 16, 64]), op=ALU.subtract),
                                 reads=["ccn"], writes=["ex"])
                            P.op("act", lambda e: e.activation(out=ex[:, :], in_=ex[:, :], func=AF.Exp, scale=1.0 / 16), reads=["ex"], writes=["ex"])
                            P.op("dve", lambda e, c=c: e.tensor_tensor(out=qd0[:, c, :], in0=qd[:, :], in1=evmask, op=ALU.mult),
                                 reads=["qd", "cb"], writes=["qd0"])
                            P.op("dve", lambda e, c=c: e.tensor_tensor(out=qd1[:, c, :], in0=qd[:, :], in1=qd0[:, c, :], op=ALU.subtract),
                                 reads=["qd", "qd0"], writes=["qd1"])
                            P.op("dve", lambda e, c=c: e.scalar_tensor_tensor(out=qu[:, :], in0=gqT[:, c, :], scalar=0.125, in1=sp_[:, :], op0=ALU.mult, op1=ALU.mult),
                                 reads=["gqT", "sp_"], writes=["qu"])
                            P.op("dve", lambda e, c=c: e.tensor_tensor(out=kd[:, :], in0=gkT[:, c, :], in1=sp_[:, :], op=ALU.mult),
                                 reads=["gkT", "sp_"], writes=["kd"])
                            P.op("dve", lambda e, c=c: e.scalar_tensor_tensor(out=qG[:, c, :], in0=gqT[:, c, :], scalar=0.125, in1=Gn[:, :], op0=ALU.mult, op1=ALU.mult),
                                 reads=["gqT", "Gn"], writes=["qG"])
                            P.op("dve", lambda e, c=c: e.tensor_tensor(out=klT[:, :], in0=gkT[:, c, :], in1=ex[:, :], op=ALU.mult),
                                 reads=["gkT", "ex"], writes=["klT"])
                            if KSTOP == 'g2':
                                return 'stop'
                            b = nb()

                            def fn(e, b=b):
                                ins = None
                                pv = banks[b][:, :].bitcast(BF16)
                                for i in range(NT):
                                    ins = e.transpose(out=pv[:, i * 128:(i + 1) * 128], in_=klT[:, i * 128:(i + 1) * 128], identity=ident)
                                return ins
                            P.op("pe", fn, reads=["klT", "cb"], writes=["ps%d" % b])
                            P.op("dve", lambda e, b=b: e.tensor_copy(out=kl_tm[:, :, :], in_=banks[b][:, :].bitcast(BF16).rearrange("p (i k) -> p i k", i=NT)),
                                 reads=["ps%d" % b], writes=["kl_tm"])
                            if KSTOP == 'g3':
                                return 'stop'
                            for i0 in range(0, NT, 4):
                                for eh in range(2):
                                    ba, bb = nb(), nb()

                                    def fa(e, bk, kk, qq, i0=i0, eh=eh):
                                        ins = None
                                        for ii in range(4):
                                            i = i0 + ii
                                            ins = e.matmul(banks[bk][:, ii * 128:(ii + 1) * 128], lhsT=kk[eh * 64:(eh + 1) * 64, i * 128:(i + 1) * 128],
                                                           rhs=qq[eh * 64:(eh + 1) * 64, i * 128:(i + 1) * 128], start=True, stop=True)
                                        return ins
                                    P.op("pe", lambda e, ba=ba, fa=fa: fa(e, ba, kd, qd), reads=["kd", "qd"], writes=["ps%d" % ba])
                                    P.op("pe", lambda e, bb=bb, fa=fa: fa(e, bb, ku, qu), reads=["ku", "qu"], writes=["ps%d" % bb])
                                    P.op("dve", lambda e, ba=ba: e.tensor_tensor(out=t1[:, :].rearrange("p (a t) -> p a t", a=4),
                                         in0=banks[ba][:, :].rearrange("p (a t) -> p a t", a=4), in1=ml_b.unsqueeze(1).to_broadcast([128, 4, 128]), op=ALU.mult),
                                         reads=["ps%d" % ba, "cb"], writes=["t1"])
                                    P.op("dve", lambda e, bb=bb: e.tensor_tensor(out=t2[:, :].rearrange("p (a t) -> p a t", a=4),
                                         in0=banks[bb][:, :].rearrange("p (a t) -> p a t", a=4), in1=mu_b.unsqueeze(1).to_broadcast([128, 4, 128]), op=ALU.mult),
                                         reads=["ps%d" % bb, "cb"], writes=["t2"])
                                    P.op("dve", lambda e, i0=i0, c=c, eh=eh: e.tensor_tensor(out=ST[:, i0:i0 + 4, 2 * c + eh, :],
                                         in0=t1[:, :].rearrange("p (i t) -> p i t", i=4), in1=t2[:, :].rearrange("p (i t) -> p i t", i=4), op=ALU.add),
                                         reads=["t1", "t2"], writes=["ST"])
                            if KSTOP == 'g4':
                                return 'stop'
                            for n0 in range(0, 16, 4):
                                bev, bod = nb(), nb()

                                def fu(e, n0=n0, c=c, par=0, bk=bev):
                                    ins = None
                                    r0 = par * 64
                                    for nn in range(2):
                                        n = n0 + 2 * nn + par
                                        ins = e.matmul(banks[bk][:, nn * 256:(nn + 1) * 256], lhsT=kl_tm[r0:r0 + 64, n // 2, :],
                                                       rhs=gv_tm[r0:r0 + 64, n // 2, c * 256:(c + 1) * 256], start=True, stop=True)
                                    return ins
                                P.op("pe", lambda e, fu=fu, bev=bev: fu(e, par=0, bk=bev), reads=["kl_tm", "gv_tm"], writes=["ps%d" % bev])
                                P.op("pe", lambda e, fu=fu, bod=bod: fu(e, par=1, bk=bod), reads=["kl_tm", "gv_tm"], writes=["ps%d" % bod])
                                for n in range(n0, n0 + 4):
                                    par = n % 2
                                    bk = bev if par == 0 else bod
                                    nn = (n - n0) // 2
                                    for eh in range(2):
                                        r0 = eh * 64
                                        P.op("dve", lambda e, bk=bk, nn=nn, n=n, eh=eh, r0=r0, c=c: e.scalar_tensor_tensor(
                                            out=S_f2[r0:r0 + 64, n % 2, c, :], in0=S_f2[r0:r0 + 64, (n + 1) % 2, c, :], scalar=elast[r0:r0 + 64, c, n:n + 1],
                                            in1=banks[bk][r0:r0 + 64, nn * 256 + eh * 128:nn * 256 + (eh + 1) * 128], op0=ALU.mult, op1=ALU.add),
                                            reads=["S_f%d" % ((n + 1) % 2), "elast", "ps%d" % bk], writes=["S_f%d" % (n % 2)])
                                    P.op("act", lambda e, n=n, c=c: e.activation(out=S_bf[:, n, c, :], in_=S_f2[:, n % 2, c, :], func=AF.Copy),
                                         reads=["S_f%d" % (n % 2)], writes=["S_bf"])
                            P.op("act", lambda e, c=c: e.activation(out=snd[:, XF_S + c * 128:XF_S + (c + 1) * 128], in_=S_f2[:, 1, c, :], func=AF.Copy),
                                 reads=["S_f1"], writes=["snd_s%d" % c])
                    if l == 0:
                        dbg("snd", snd[:, :], [128, NXF], F32, ["snd_s0", "snd_s1", "snd_nlf"])
                        dbg("ST", ST[:, :, :, :].rearrange("p a b t -> p (a b t)"), [128, NT * 512], BF16, ["ST"])
                        dbg("Sbf", S_bf[:, :, :, :].rearrange("p a b t -> p (a b t)"), [128, 16 * 256], BF16, ["S_bf"])
                        dbg("qd0", qd0[:, :, :].rearrange("p a t -> p (a t)"), [128, 2 * T], BF16, ["qd0"])
                        dbg("qd1", qd1[:, :, :].rearrange("p a t -> p (a t)"), [128, 2 * T], BF16, ["qd1"])
                        dbg("qG", qG[:, :, :].rearrange("p a t -> p (a t)"), [128, 2 * T], BF16, ["qG"])
                    P.fence()
                    if KSTOP == 'gla':
                        return 'stop'
                    P.dma("sp", lambda e: e.dma_start(out=xs_f2[:, :], in_=snd[:, XF_S:XF_S + 256]), "snd2", reads=["snd_s0", "snd_s1"], writes=["xs_f2"])
                    P.dma("pool", lambda e: e.collective_compute("AllGather", ALU.bypass, replica_groups=[[2 * i, 2 * i + 1] for i in range(NRUN // 2)],
                          ins=[xs_f2[:, :]], outs=[xr_f2[:, :]]), "cc_f2", reads=["xs_f2"], writes=["xr_f2"], inc=1)
                    rcv = gsb("rcv", [128, NXF], F32)
                    P.dma("sp", lambda e: e.dma_start(out=rcv[:, 0:XF_S], in_=xr_f1[0:128, :]), "rcv1", reads=["xr_f1"], writes=["rcv1"])
                    S0_bf = gsb("S0_bf", [128, 2, 128], BF16)
                    P.op("dve", lambda e: e.tensor_scalar(out=ufix[:, :, 0:2], in0=rcv[:, XF_A:XF_A + 8].rearrange("p (j t) -> p j t", j=4), scalar1=role,
                         scalar2=None, op0=ALU.mult), reads=["rcv1", "pc"], writes=["ufix"])
                    P.op("dve", lambda e: e.tensor_scalar(out=pfix[:, :, 0:16], in0=rcv[:, XF_B:XF_B + 64].rearrange("p (j t) -> p j t", j=4), scalar1=role,
                         scalar2=None, op0=ALU.mult), reads=["rcv1", "pc"], writes=["pfix"])
                    if KSTOP == 'xchg':
                        return 'stop'
                    fx = gsb("fx", [128, 8], F32)
                    pf1 = gsb("pf1", [128, 32], F32)
                    pf2 = gsb("pf2", [128, 32], F32)
                    dfx = gsb("dfx", [128, 16], BF16)
                    for j in range(4):
                        P.op("dve", lambda e, j=j: e.tensor_scalar(out=fx[:, 0:2], in0=ufix[:, j, 2:4], scalar1=pkl[:, PK_CW + j * 3 + 2:PK_CW + j * 3 + 3],
                             scalar2=None, op0=ALU.mult), reads=["ufix", pkk], writes=["fx"])
                        P.op("dve", lambda e, j=j: e.scalar_tensor_tensor(out=fx[:, 0:2], in0=ufix[:, j, 1:3], scalar=pkl[:, PK_CW + j * 3 + 1:PK_CW + j * 3 + 2],
                             in1=fx[:, 0:2], op0=ALU.mult, op1=ALU.add), reads=["ufix", pkk, "fx"], writes=["fx"])
                        P.op("dve", lambda e, j=j: e.scalar_tensor_tensor(out=fx[:, 0:2], in0=ufix[:, j, 0:2], scalar=pkl[:, PK_CW + j * 3:PK_CW + j * 3 + 1],
                             in1=fx[:, 0:2], op0=ALU.mult, op1=ALU.add), reads=["ufix", pkk, "fx"], writes=["fx"])
                        P.op("dve", lambda e, j=j: e.tensor_tensor(out=yT[:, j, 0:2], in0=fx[:, 0:2], in1=fixA_cb[:, j, :], op=ALU.mult),
                             reads=["fx", "fixA_cb"], writes=["yT%d" % j])
                    for j in range(4):
                        cur = pfix[:, j, :]
                        curk = "pfix"
                        pp = [(pf1, "pf1"), (pf2, "pf2")]
                        lo = 0
                        for st in range(j + 1):
                            sh = 1 << st
                            dst, dstk = pp[st % 2]
                            lo2 = lo + sh
                            P.op("dve", lambda e, cur=cur, dst=dst, sh=sh, lo2=lo2: e.tensor_tensor(out=dst[:, lo2:32], in0=cur[:, lo2:32],
                                 in1=cur[:, lo2 - sh:32 - sh], op=ALU.add), reads=[curk], writes=[dstk])
                            cur, curk, lo = dst[:, :], dstk, lo2
                        mk = "pf1" if curk == "pf2" else "pf2"
                        mt = pf1 if curk == "pf2" else pf2
                        P.op("dve", lambda e, cur=cur, j=j, mt=mt: e.tensor_tensor(out=mt[:, 0:16], in0=cur[:, 16:32],
                             in1=pc[:, PC_INVC + j * 16:PC_INVC + (j + 1) * 16], op=ALU.mult), reads=[curk, "pc"], writes=[mk])
                        P.op("dve", lambda e, mt=mt, j=j: e.tensor_tensor(out=dfx[:, :], in0=mt[:, 0:16], in1=pfix[:, j, 16:32], op=ALU.subtract),
                             reads=[mk, "pfix"], writes=["dfx"])
                        b = nb()
                        P.op("pe", lambda e, b=b, j=j: e.matmul(banks[b][:, 0:16], lhsT=poolw[l][:, j, :], rhs=dfx[:, :], start=True, stop=True),
                             reads=["dfx", "poolw%d" % l], writes=["ps%d" % b])
                        P.op("act", lambda e, b=b, j=j: e.activation(out=yT[:, 4 + j, 0:16], in_=banks[b][:, 0:16], func=AF.Copy,
                             scale=pkl[:, PK_PSC + j:PK_PSC + j + 1]), reads=["ps%d" % b, pkk], writes=["yT%d" % (4 + j)])
                    def emit_s0():
                        P.dma("sp", lambda e: e.dma_start(out=rcv[:, XF_S:XF_S + 256], in_=xr_f2[0:128, :]), "rcv2", reads=["xr_f2"], writes=["rcv2"])
                        P.op("dve", lambda e: e.tensor_scalar(out=S0_bf[:, :, :].rearrange("p c v -> p (c v)"), in0=rcv[:, XF_S:XF_S + 256], scalar1=role,
                             scalar2=None, op0=ALU.mult), reads=["rcv2", "pc"], writes=["S0_bf"])
                    yc2 = [gsb("yc%d" % k, [128, 512], BF16) for k in range(2)]
                    ssg2 = [gsb("ssg%d" % k, [128, 8], F32) for k in range(2)]
                    junk = gsb("junk", [128, 128], BF16)
                    def gla_out_tile(i):
                        b = 6
                        yc, ssg = yc2[i % 2], ssg2[i % 2]
                        yck, ssk = "yc%d" % (i % 2), "ssg%d" % (i % 2)

                        def fo(e, b=b, i=i):
                            ins = None
                            for hh in range(4):
                                c, eh = hh // 2, hh % 2
                                r0 = eh * 64
                                o = banks[b][:, hh * 128:(hh + 1) * 128]
                                tok = slice(i * 128, (i + 1) * 128)
                                e.matmul(o, lhsT=ST[:, i, hh, :], rhs=gv_tm[:, i, hh * 128:(hh + 1) * 128], start=True, stop=False)
                                if i > 0:
                                    e.matmul(o, lhsT=qd0[r0:r0 + 64, c, tok], rhs=S_bf[r0:r0 + 64, 2 * i - 1, c, :], start=False, stop=False)
                                e.matmul(o, lhsT=qd1[r0:r0 + 64, c, tok], rhs=S_bf[r0:r0 + 64, 2 * i, c, :], start=False, stop=False)
                                ins = e.matmul(o, lhsT=qG[r0:r0 + 64, c, tok], rhs=S0_bf[r0:r0 + 64, c, :], start=False, stop=True)
                            return ins
                        P.op("pe", fo, reads=["ST", "gv_tm", "qd0", "qd1", "qG", "S_bf", "S0_bf"], writes=["ps%d" % b])
                        for hh in range(4):
                            P.op("act", lambda e, b=b, hh=hh, ssg=ssg: e.activation(out=junk[:, :], in_=banks[b][:, hh * 128:(hh + 1) * 128], func=AF.Square,
                                 accum_out=ssg[:, hh:hh + 1]), reads=["ps%d" % b], writes=["junk", ssk])
                        P.op("act", lambda e, ssg=ssg: e.activation(out=ssg[:, 4:8], in_=ssg[:, 0:4], func=AF.Sqrt, scale=1.0 / 128, bias=eps_col),
                             reads=[ssk, "epsc"], writes=[ssk])
                        P.op("dve", lambda e, ssg=ssg: e.reciprocal(out=ssg[:, 4:8], in_=ssg[:, 4:8]), reads=[ssk], writes=[ssk])
                        for hh in range(4):
                            P.op("dve", lambda e, b=b, hh=hh, i=i, yc=yc, ssg=ssg: e.scalar_tensor_tensor(out=yc[:, hh * 128:(hh + 1) * 128], in0=banks[b][:, hh * 128:(hh + 1) * 128],
                                 scalar=ssg[:, 4 + hh:5 + hh], in1=sgg[:, i, hh * 128:(hh + 1) * 128], op0=ALU.mult, op1=ALU.mult),
                                 reads=["ps%d" % b, ssk, "sgg"], writes=[yck])

                    def gla_out_tile_b(i):
                        yc = yc2[i % 2]
                        yck = "yc%d" % (i % 2)
                        b2 = 7

                        def ft(e, b2=b2, yc=yc):
                            ins = None
                            pv = banks[b2][:, :].bitcast(BF16)
                            for hh in range(4):
                                ins = e.transpose(out=pv[:, hh * 128:(hh + 1) * 128], in_=yc[:, hh * 128:(hh + 1) * 128], identity=ident)
                            return ins
                        P.op("pe", ft, reads=[yck, "cb"], writes=["ps%d" % b2])
                        P.op("act", lambda e, b2=b2, i=i: e.activation(out=yT[:, 8:12, i * 128:(i + 1) * 128],
                             in_=banks[b2][:, :].bitcast(BF16)[:, 0:512].rearrange("p (h t) -> p h t", h=4), func=AF.Copy),
                             reads=["ps%d" % b2], writes=["yT8", "yT9", "yT10", "yT11"])
                    if KSTOP == 'glao':
                        return 'stop'
                    fsb_ = gsb("fsb", [128, 128], F32)
                    pin = gsb("pin", [128, 64], F32)
                    fbias = gsb("fbias", [128, 2, 64], F32)
                    nlfp = rcv[:, XF_NLF:XF_NLF + 32]
                    b = nb()

                    def ff_(e, b=b):
                        e.matmul(banks[b][:, 0:32], lhsT=tri_f, rhs=nlfp, start=True, stop=True)
                        e.matmul(banks[b][:, 32:64], lhsT=tri_f, rhs=nlf[:, :], start=True, stop=True)
                        e.matmul(banks[b][:, 64:96], lhsT=ones_f, rhs=nlfp, start=True, stop=True)
                        return e.matmul(banks[b][:, 96:128], lhsT=ones_f, rhs=nlf[:, :], start=True, stop=True)
                    P.op("pe", ff_, reads=["cf", "rcv1", "nlf"], writes=["ps%d" % b])
                    P.op("dve", lambda e, b=b: e.tensor_copy(out=fsb_[:, :], in_=banks[b][:, 0:128]), reads=["ps%d" % b], writes=["fsb"])
                    for hh in range(4):
                        P.op("dve", lambda e, hh=hh: e.tensor_tensor_scan(out=pin[:, :].rearrange("p (j h) -> p j h", h=4)[:, :, hh],
                             data0=one_col.to_broadcast([128, 16]), data1=fsb_[:, 64:128].rearrange("p (j h) -> p j h", h=4)[:, :, hh],
                             initial=0.0, op0=ALU.mult, op1=ALU.add), reads=["fsb", "onec"], writes=["pin"])
                    P.op("dve", lambda e: e.tensor_tensor(out=pin[:, :], in0=pin[:, :], in1=fsb_[:, 64:128], op=ALU.subtract), reads=["pin", "fsb"], writes=["pin"])
                    P.op("dve", lambda e: e.tensor_tensor(out=fsb_[:, 0:64], in0=fsb_[:, 0:64], in1=pin[:, :], op=ALU.add), reads=["pin", "fsb"], writes=["fsb"])
                    for g in range(2):
                        jr = 8 + 4 * g + 2
                        P.op("dve", lambda e, g=g, jr=jr: e.tensor_tensor(out=fbias[:, g, :].rearrange("p (j h) -> p j h", h=4),
                             in0=fsb_[:, 0:64].rearrange("p (j h) -> p j h", h=4),
                             in1=pin[:, jr * 4:(jr + 1) * 4].unsqueeze(1).to_broadcast([128, 16, 4]), op=ALU.subtract),
                             reads=["fsb", "pin"], writes=["fbias"])
                        P.op("dve", lambda e, g=g: e.tensor_scalar(out=fbias[:, g, 0:32], in0=fbias[:, g, 0:32], scalar1=negrole, scalar2=None, op0=ALU.add),
                             reads=["fbias", "pc"], writes=["fbias"])
                    kp = [gsb("kp%d" % i, [128, T], BF16) for i in range(2)]
                    ko = [gsb("ko%d" % i, [128, T], BF16) for i in range(2)]
                    vp = [gsb("vp%d" % i, [128, NT, 128], BF16) for i in range(2)]
                    vo = [gsb("vo%d" % i, [128, NT, 128], BF16) for i in range(2)]
                    PT = [gsb("PT%d" % i, [128, 512], BF16) for i in range(5)]
                    rden = gsb("rden", [128, 512], F32)
                    numS = gsb("numS", [128, 512], F32)
                    ptn = 0
                    scn = [0]
                    scale = 128.0 ** -0.5
                    for hh in range(4):
                        s = hh % 2
                        P.dma("sp", lambda e, s=s, hh=hh: e.dma_start(out=kp[s][:, :], in_=xr_kv[0:128, hh * 1024:(hh + 1) * 1024]), "kp%d" % s,
                              reads=["xr_kv"], writes=["kp%d" % s])
                        P.dma("sp", lambda e, s=s, hh=hh: e.dma_start(out=ko[s][:, :], in_=xs_kv[:, hh * 1024:(hh + 1) * 1024]), "ko%d" % s,
                              reads=["xs_k%d" % hh], writes=["ko%d" % s])
                        P.dma("sp", lambda e, s=s, hh=hh: e.dma_start(out=vp[s][:, :, :].rearrange("p i d -> p (i d)"),
                              in_=xr_kv[0:128, 4096 + hh * 1024:4096 + (hh + 1) * 1024]), "vp%d" % s, reads=["xr_kv"], writes=["vp%d" % s])
                        P.dma("sp", lambda e, s=s, hh=hh: e.dma_start(out=vo[s][:, :, :].rearrange("p i d -> p (i d)"),
                              in_=xs_kv[:, 4096 + hh * 1024:4096 + (hh + 1) * 1024]), "vo%d" % s, reads=["xs_v%d" % i for i in range(NT)], writes=["vo%d" % s])
                        for g in range(2):
                            q0 = g * 512
                            bnum, bden = 0, 1
                            nJ = 8 + 4 * g + 4
                            pend = []
                            for J in range(nJ):
                                r = J - (8 + 4 * g)
                                nc0 = max(r, 0) * 128
                                if J < 8:
                                    kblk, kkey = kp[s][:, J * 128:(J + 1) * 128], "kp%d" % s
                                    vblk, vkey = vp[s][:, J, :], "vp%d" % s
                                else:
                                    kblk, kkey = ko[s][:, (J - 8) * 128:(J - 7) * 128], "ko%d" % s
                                    vblk, vkey = vo[s][:, J - 8, :], "vo%d" % s
                                bs = 2 + scn[0] % 4
                                scn[0] += 1

                                def fs(e, bs=bs, kblk=kblk, nc0=nc0, r=r, q0=q0, hh=hh):
                                    ins = e.matmul(banks[bs][:, nc0:512], lhsT=kblk, rhs=qT[:, hh, q0 + nc0:q0 + 512], start=True, stop=(r < 0))
                                    if r >= 0:
                                        ins = e.matmul(banks[bs][:, nc0:nc0 + 128], lhsT=ident, rhs=mneg_b, start=False, stop=True)
                                    return ins
                                P.op("pe", fs, reads=[kkey, "qT", "cb"], writes=["ps%d" % bs])
                                pt = PT[ptn % 5]
                                ptk = "PT%d" % (ptn % 5)
                                ptn += 1
                                P.op("act", lambda e, bs=bs, pt=pt, nc0=nc0, g=g, J=J, hh=hh: e.activation(out=pt[:, nc0:512], in_=banks[bs][:, nc0:512], func=AF.Exp,
                                     scale=scale, bias=fbias[:, g, J * 4 + hh:J * 4 + hh + 1]), reads=["ps%d" % bs, "fbias"], writes=[ptk])

                                def pv(vblk=vblk, vkey=vkey, pt=pt, ptk=ptk, nc0=nc0, J=J, bnum=bnum, bden=bden, nJ=nJ):
                                    P.op("pe", lambda e: e.matmul(banks[bnum][:, nc0:512], lhsT=vblk, rhs=pt[:, nc0:512],
                                         start=(J == 0), stop=(J == nJ - 1)), reads=[vkey, ptk], writes=["ps%d" % bnum])
                                    P.op("pe", lambda e: e.matmul(banks[bden][:, nc0:512], lhsT=ones_b, rhs=pt[:, nc0:512],
                                         start=(J == 0), stop=(J == nJ - 1)), reads=["cb", ptk], writes=["ps%d" % bden])
                                pend.append(pv)
                                if len(pend) > 3:
                                    pend.pop(0)()
                            while pend:
                                pend.pop(0)()
                            P.op("act", lambda e, bden=bden: e.activation(out=rden[:, :], in_=banks[bden][:, :], func=AF.Copy), reads=["ps%d" % bden], writes=["rden"])
                            P.op("act", lambda e, bnum=bnum: e.activation(out=numS[:, :], in_=banks[bnum][:, :], func=AF.Copy), reads=["ps%d" % bnum], writes=["numS"])
                            P.op("dve", lambda e: e.reciprocal(out=rden[:, :], in_=rden[:, :]), reads=["rden"], writes=["rden"])
                            P.op("dve", lambda e, hh=hh, q0=q0: e.tensor_tensor(out=yT[:, 12 + hh, q0:q0 + 512], in0=numS[:, :], in1=rden[:, :], op=ALU.mult),
                                 reads=["numS", "rden"], writes=["yT%d" % (12 + hh)])
                            for kind, ti in GLA_SCHED[hh * 2 + g]:
                                if kind == 'S':
                                    emit_s0()
                                elif kind == 'A':
                                    gla_out_tile(ti)
                                else:
                                    gla_out_tile_b(ti)
                P.fence()
            with contextlib.ExitStack() as fstack:
                h = fstack.enter_context(nc.sbuf_tensor("h_l%d" % l, [128, NT, D], F32))
                xin_t = fstack.enter_context(nc.sbuf_tensor("xin_l%d" % l, [128, 2, 512], F32))
                xin = [xin_t[:, 0, :], xin_t[:, 1, :]]
                sgt = [fstack.enter_context(nc.sbuf_tensor("sgt%d_l%d" % (i, l), [128, 512], F32)) for i in range(2)]
                uT = fstack.enter_context(nc.sbuf_tensor("uTf_l%d" % l, [128, 16, T], BF16))
                wo = w_out_d[l].rearrange("(c p) n -> p c n", p=128)
                ytkeys = ["yT%d" % j for j in range(16)]
                if KDBG and l == 0:
                    P.dma("sp", lambda e: e.dma_start(out=dbg_yT[:, :], in_=yT[:, :, :].rearrange("p a t -> p (a t)")), "dbg2", reads=ytkeys, writes=["dbg_yT"])
                xn_i = 0
                for cg in range(4):
                    slab, wk = load_slab([(lambda s: s[:, :, :], wo[:, :, cg * 512:(cg + 1) * 512])])
                    for i in range(NT):
                        b = nb()

                        def fo2(e, b=b, slab=slab, i=i):
                            ins = None
                            for kc in range(16):
                                ins = e.matmul(banks[b][:, :], lhsT=yT[:, kc, i * 128:(i + 1) * 128], rhs=slab[:, kc, :], start=(kc == 0), stop=(kc == 15))
                            return ins
                        P.op("pe", fo2, reads=wk + ytkeys, writes=["ps%d" % b])
                        xb_ = xin[xn_i % 2]
                        xk = "xin%d" % (xn_i % 2)
                        xn_i += 1
                        P.dma("sp", lambda e, xb_=xb_, i=i, cg=cg: e.dma_start(out=xb_, in_=src_d[i * 128:(i + 1) * 128, cg * 512:(cg + 1) * 512]),
                              xk, reads=["%s%d" % (srckey, i)], writes=[xk])
                        P.op("dve", lambda e, b=b, xb_=xb_, i=i, cg=cg: e.tensor_tensor(out=h[:, i, cg * 512:(cg + 1) * 512], in0=xb_, in1=banks[b][:, :], op=ALU.add),
                             reads=["ps%d" % b, xk], writes=["h%d" % i])
                if KDBG and l == 0:
                    P.dma("sp", lambda e: e.dma_start(out=dbg_h1[:, :], in_=h[:, :, :].rearrange("p a t -> p (a t)")), "dbg3", reads=["h%d" % i for i in range(NT)], writes=["dbg_h1"])
                if KSTOP == 'out':
                    return 'stop'
                rmsnorm_to_uT(uT, l, PK_GF, lambda i: (h[:, i, :], "h%d" % i),
                              xin_t[:, :, :].rearrange("p a b -> p (a b)").bitcast(BF16), ["xin0", "xin1"])
                wg = w_gate_d[l].rearrange("(c p) n -> p c n", p=128)
                wu = w_up_d[l].rearrange("(c p) n -> p c n", p=128)
                wd = w_down_d[l].rearrange("(f p) n -> p f n", p=128)
                actT = yT
                f0 = 0
                sgn = 0
                for nf in (16, 16, 12):
                    for sub in range(0, nf, 2):
                        cs = (f0 + sub) * 128
                        gslab, gk_ = load_slab([(lambda s: s[:, :, 0:256], wg[:, :, cs:cs + 256]), (lambda s: s[:, :, 256:512], wu[:, :, cs:cs + 256])])
                        uslab, uk_ = gslab, gk_
                        for m in range(2):
                            fl = sub + m
                            for hf in range(2):
                                bg, bu = nb(), nb()
                                fm_group(uT, bg, gslab, gk_, m * 128, 128, hf)
                                fm_group(uT, bu, uslab, uk_, 256 + m * 128, 128, hf)
                                sg_ = sgt[sgn % 2]
                                sgk = "sgt%d" % (sgn % 2)
                                sgn += 1
                                P.op("act", lambda e, bg=bg, sg_=sg_: e.activation(out=sg_[:, :], in_=banks[bg][:, :], func=AF.Silu),
                                     reads=["ps%d" % bg], writes=[sgk])
                                P.op("dve", lambda e, bu=bu, sg_=sg_, fl=fl, hf=hf: e.tensor_tensor(out=actT[:, fl, hf * 512:(hf + 1) * 512], in0=sg_[:, :],
                                     in1=banks[bu][:, :], op=ALU.mult), reads=["ps%d" % bu, sgk], writes=["yT%d" % fl])
                    for cg in range(4):
                        slab, wk = load_slab([(lambda s, nf=nf: s[:, 0:nf, :], wd[:, f0:f0 + nf, cg * 512:(cg + 1) * 512])])
                        for i in range(NT):
                            b = nb()

                            def fd(e, b=b, slab=slab, i=i, nf=nf):
                                ins = None
                                for fl in range(nf):
                                    ins = e.matmul(banks[b][:, :], lhsT=actT[:, fl, i * 128:(i + 1) * 128], rhs=slab[:, fl, :], start=(fl == 0), stop=(fl == nf - 1))
                                return ins
                            P.op("pe", fd, reads=wk + ["yT%d" % fl for fl in range(nf)], writes=["ps%d" % b])
                            P.op("dve", lambda e, b=b, i=i, cg=cg: e.tensor_tensor(out=h[:, i, cg * 512:(cg + 1) * 512], in0=h[:, i, cg * 512:(cg + 1) * 512],
                                 in1=banks[b][:, :], op=ALU.add), reads=["ps%d" % b, "h%d" % i], writes=["h%d" % i])
                    f0 += nf
                otoks = []
                for i in range(NT):
                    otoks.append(P.dma("sp", lambda e, i=i: e.dma_start(out=dst_d[i * 128:(i + 1) * 128, :], in_=h[:, i, :]), "spill%d" % i,
                                       reads=["h%d" % i], writes=["hs%d" % i if l < DRUN - 1 else "outd%d" % i]))
                P.fence()
                if l == DRUN - 1:
                    P.final_wait("sp", otoks)
        stopped = False
        for l in range(DRUN):
            if emit_layer(l) == 'stop':
                stopped = True
                break
        if stopped:
            P.fence()
            tok = P.dma("sp", lambda e: e.dma_start(out=out_d[0:128, 0:NPK], in_=pk[0][:, :]), "stopout", reads=["pk0"], writes=["outstop"])
            P.final_wait("sp", [tok])
        P.build()
    return nc


_CACHE = {}


def _consts():
    import ml_dtypes
    cbf = np.zeros((128, NCB), np.float32)
    s = np.arange(128)
    cbf[:, CB_ID:CB_ID + 128] = np.eye(128, dtype=np.float32)
    cbf[:, CB_ONE:CB_ONE + 128] = 1.0
    same = (s[:, None] // 64) == (s[None, :] // 64)
    cbf[:, CB_ML:CB_ML + 128] = (same & (s[:, None] <= s[None, :])).astype(np.float32)
    cbf[:, CB_MU:CB_MU + 128] = (same & (s[:, None] > s[None, :])).astype(np.float32)
    cbf[:, CB_MN:CB_MN + 128] = np.where(s[:, None] > s[None, :], NEG, 0.0)
    t = np.arange(1024)
    cbf[:, CB_CM:CB_CM + 1024] = (t % 64 != 0).astype(np.float32)[None, :]
    cbf[:, CB_EV:CB_EV + 1024] = ((t // 64) % 2 == 0).astype(np.float32)[None, :]
    cf = np.zeros((128, NCF), np.float32)
    cf[:, CF_TRI:CF_TRI + 128] = (s[:, None] <= s[None, :]).astype(np.float32)
    cf[:, CF_ONE:CF_ONE + 128] = 1.0
    pcs = []
    for r in range(2):
        pc = np.zeros((128, NPC), np.float32)
        pc[:, PC_ROLE] = float(r)
        pc[:, PC_NEGR] = NEG * (1 - r)
        for j in range(4):
            w = 2 << j
            tt = np.arange(16)
            cnt = np.minimum(tt + 1, w) if r == 0 else np.full(16, w)
            pc[:, PC_INVC + j * 16:PC_INVC + (j + 1) * 16] = (1.0 / cnt.astype(np.float32))[None, :]
        pcs.append(pc)
    return cbf, cf, pcs


def _pack_params(conv_w, pool_scale, gla_w_decay, gla_b_decay, gla_out_g, fox_q_g, fox_k_g, fox_b_f, norm_mix_g, norm_ffn_g):
    pk = np.zeros((DEPTH, 128, NPK), np.float32)
    for l in range(DEPTH):
        cw = conv_w[l][:, 0, :]
        pk[l, :, PK_CW:PK_CW + 12] = cw.reshape(3, 4, 128).transpose(2, 1, 0).reshape(128, 12)
        pk[l, :, PK_PSC:PK_PSC + 4] = pool_scale[l].reshape(4, 128).T
        pk[l, :, PK_BD:PK_BD + 2] = gla_b_decay[l].reshape(2, 128).T
        pk[l, :, PK_QG] = fox_q_g[l]
        pk[l, :, PK_KG] = fox_k_g[l]
        pk[l, :, PK_BF:PK_BF + 32] = np.tile(fox_b_f[l], 8)[None, :]
        pk[l, :, PK_GOG:PK_GOG + 128] = gla_out_g[l][None, :]
        pk[l, 0:16, PK_WD:PK_WD + 256] = gla_w_decay[l]
        pk[l, :, PK_GM:PK_GM + 16] = norm_mix_g[l].reshape(16, 128).T
        pk[l, :, PK_GF:PK_GF + 16] = norm_ffn_g[l].reshape(16, 128).T
    return pk


def kernel(x, norm_mix_g, w_in, conv_w, pool_w, pool_scale, gla_w_decay, gla_b_decay, gla_out_g,
           fox_q_g, fox_k_g, fox_b_f, w_out, norm_ffn_g, w_gate, w_up, w_down):
    f = lambda a: np.ascontiguousarray(np.asarray(a, dtype=np.float32))
    x = f(x)
    if "nc" not in _CACHE:
        _CACHE["nc"] = build_program()
    nc = _CACHE["nc"]
    cbf, cf, pcs = _consts()
    pk = _pack_params(f(conv_w), f(pool_scale), f(gla_w_decay), f(gla_b_decay), f(gla_out_g), f(fox_q_g), f(fox_k_g), f(fox_b_f),
                      f(norm_mix_g), f(norm_ffn_g))
    poolw = np.ascontiguousarray(f(pool_w).transpose(0, 2, 1, 3).reshape(DEPTH, 128, 512))
    w_in, w_out, w_gate, w_up, w_down = f(w_in), f(w_out), f(w_gate), f(w_up), f(w_down)
    in_maps = []
    for c in range(NRUN):
        b, r = c // 2, c % 2
        in_maps.append({
            "x": np.ascontiguousarray(x[b, r * T:(r + 1) * T, :]),
            "w_in": w_in, "w_out": w_out, "w_gate": w_gate, "w_up": w_up, "w_down": w_down,
            "pkp": pk, "poolwp": poolw, "cbf": cbf, "cf32": cf, "pcc": pcs[r],
        })
    t0 = _time.time()
    res = run_bass_kernel_spmd(nc, in_maps, core_ids=list(range(NRUN)))
    if os.environ.get("KVERBOSE"):
        print("run_bass_kernel_spmd seconds", _time.time() - t0)
    out = np.zeros((4, 2 * T, D), np.float32)
    for c in range(NRUN):
        b, r = c // 2, c % 2
        out[b, r * T:(r + 1) * T, :] = res.results[c]["out"]
    return out
```
